# Optimizing a Trainium2 kernel written in Bass

```python
import math
import jax, jax.numpy as jnp
from jax import lax
import numpy as np

D_MODEL = 1024
BATCH = 16
SEQ = 2048
DEPTH = 2

N_MIXERS = 2
NORM_EPS = 1e-6

GLA_HEADS = 4
GLA_DK = D_MODEL // 2
GLA_DV = D_MODEL
GLA_DK_HEAD = GLA_DK // GLA_HEADS
GLA_DV_HEAD = GLA_DV // GLA_HEADS
GLA_RANK = 16
GLA_TAU = 16.0
GLA_CHUNK = 64
GLA_IN = 2 * GLA_DK + GLA_DV + GLA_RANK + GLA_DV

DSW_HEADS = 16
DSW_HEAD_DIM = D_MODEL // DSW_HEADS
DSW_WIDTH = DSW_HEADS * DSW_HEAD_DIM
DSW_GROUPS = ((128, 1), (512, 4), (2048, 16))
N_GROUPS = len(DSW_GROUPS)
DSW_IN = N_GROUPS * 3 * DSW_WIDTH + DSW_WIDTH

REL_BUCKETS = 32
REL_MAX_DIST = 2048

kernel_name = "hybrid_gla_dilated_swa_adaln"


def rmsnorm(x, g):
    x32 = x.astype(jnp.float32)
    y = x32 * lax.rsqrt(jnp.mean(x32 * x32, axis=-1, keepdims=True) + NORM_EPS)
    return y.astype(x.dtype) * g


def t5_causal_bucket(n):
    max_exact = REL_BUCKETS // 2
    nf = np.maximum(n, 1).astype(np.float32)
    large = max_exact + (np.log(nf / max_exact) / math.log(REL_MAX_DIST / max_exact)
                         * (REL_BUCKETS - max_exact)).astype(np.int32)
    large = np.minimum(large, REL_BUCKETS - 1)
    return np.where(n < max_exact, n, large).astype(np.int32)


def gla_mixer(h, w_in, w_alpha, b_alpha, norm_g, w_out):
    B, S, _ = h.shape
    H, dk, dv, C = GLA_HEADS, GLA_DK_HEAD, GLA_DV_HEAD, GLA_CHUNK
    nc = S // C
    proj = h @ w_in
    q, k, v, g_lr, r = jnp.split(
        proj, [GLA_DK, 2 * GLA_DK, 2 * GLA_DK + GLA_DV, 2 * GLA_DK + GLA_DV + GLA_RANK], axis=-1)
    log_a = jax.nn.log_sigmoid((g_lr @ w_alpha + b_alpha).astype(jnp.float32)) / GLA_TAU

    def chunks(t, d):
        return t.astype(jnp.float32).reshape(B, nc, C, H, d).transpose(0, 3, 1, 2, 4)

    q = chunks(q, dk) * (dk ** -0.5)
    k = chunks(k, dk)
    v = chunks(v, dv)
    b = jnp.cumsum(chunks(log_a, dk), axis=3)

    mid = b[:, :, :, C // 2:C // 2 + 1]
    a_intra = jnp.einsum('bhnid,bhnjd->bhnij', q * jnp.exp(b - mid), k * jnp.exp(mid - b))
    causal = jnp.tril(jnp.ones((C, C), dtype=bool))
    a_intra = jnp.where(causal, a_intra, 0.0)
    o = jnp.einsum('bhnij,bhnje->bhnie', a_intra, v)

    b_last = b[:, :, :, -1:]
    kv = jnp.einsum('bhnjd,bhnje->bhnde', k * jnp.exp(b_last - b), v)
    decay = jnp.exp(b_last[:, :, :, 0])

    def step(state, inp):
        dec, kv_n = inp
        return dec[..., None] * state + kv_n, state

    _, states = lax.scan(step, jnp.zeros((B, H, dk, dv), jnp.float32),
                         (decay.transpose(2, 0, 1, 3), kv.transpose(2, 0, 1, 3, 4)))
    o = o + jnp.einsum('bhnid,nbhde->bhnie', q * jnp.exp(b), states)

    o = o.transpose(0, 2, 3, 1, 4).reshape(B, S, H, dv)
    o = rmsnorm(o, norm_g)
    o = o.reshape(B, S, GLA_DV).astype(h.dtype) * jax.nn.silu(r)
    return o @ w_out


def dilated_group_attn(q, k, v, rel_bias, window, dil):
    B, S, H, Dh = q.shape
    span = window // dil
    L = S // dil
    nb = -(-L // span)
    Lp = nb * span

    def to_sub(t):
        t = t.reshape(B, L, dil, H, Dh).transpose(0, 2, 3, 1, 4)
        return jnp.pad(t, ((0, 0), (0, 0), (0, 0), (0, Lp - L), (0, 0)))

    def banded(t):
        tp = jnp.pad(to_sub(t), ((0, 0), (0, 0), (0, 0), (span, 0), (0, 0)))
        prev = tp[:, :, :, :Lp].reshape(B, dil, H, nb, span, Dh)
        cur = tp[:, :, :, span:].reshape(B, dil, H, nb, span, Dh)
        return jnp.concatenate([prev, cur], axis=-2)

    qs = to_sub(q).reshape(B, dil, H, nb, span, Dh)
    kb, vb = banded(k), banded(v)

    qi = np.arange(span)[:, None]
    kj = np.arange(2 * span)[None, :]
    steps = qi + span - kj
    in_window = (steps >= 0) & (steps <= span)
    bucket = t5_causal_bucket(np.clip(steps, 0, span) * dil)
    bias = rel_bias[bucket].transpose(2, 0, 1).astype(jnp.float32)
    first_block = (np.arange(nb) == 0)[:, None, None]
    mask = in_window[None] & (~first_block | (kj >= span)[None])

    logits = jnp.einsum('bdhnqe,bdhnke->bdhnqk', qs, kb).astype(jnp.float32) * (Dh ** -0.5)
    logits = jnp.where(mask, logits + bias[:, None], -jnp.inf)
    lse = jax.nn.logsumexp(logits, axis=-1)
    p = jnp.exp(logits - lse[..., None])
    o = jnp.einsum('bdhnqk,bdhnke->bdhnqe', p.astype(vb.dtype), vb)

    o = o.reshape(B, dil, H, Lp, Dh)[:, :, :, :L].transpose(0, 3, 1, 2, 4).reshape(B, S, H, Dh)
    lse = lse.reshape(B, dil, H, Lp)[:, :, :, :L].transpose(0, 3, 1, 2).reshape(B, S, H)
    return o, lse


def dilated_mixer(h, w_in, w_out, rel_bias):
    B, S, _ = h.shape
    proj = h @ w_in
    qkv = proj[..., :N_GROUPS * 3 * DSW_WIDTH].reshape(B, S, N_GROUPS, 3, DSW_HEADS, DSW_HEAD_DIM)
    gate = proj[..., N_GROUPS * 3 * DSW_WIDTH:]
    outs, lses = [], []
    for g, (window, dil) in enumerate(DSW_GROUPS):
        o, lse = dilated_group_attn(qkv[:, :, g, 0], qkv[:, :, g, 1], qkv[:, :, g, 2],
                                    rel_bias, window, dil)
        outs.append(o)
        lses.append(lse)
    w = jax.nn.softmax(jnp.stack(lses, axis=0), axis=0)
    o = jnp.sum(w[..., None] * jnp.stack(outs, axis=0).astype(jnp.float32), axis=0)
    o = o.reshape(B, S, DSW_WIDTH).astype(h.dtype) * jax.nn.silu(gate)
    return o @ w_out


def setup_inputs(seed: int = 0) -> dict:
    key = jax.random.key(seed)
    ks = jax.random.split(key, 16)
    n_gla = (DEPTH + N_MIXERS - 1) // N_MIXERS
    n_dsw = DEPTH // N_MIXERS
    nrm = jax.random.normal
    f32 = jnp.float32
    return {
        "x": nrm(ks[0], (BATCH, SEQ, D_MODEL), f32),
        "c": nrm(ks[1], (BATCH, D_MODEL), f32),
        "ada_w": nrm(ks[2], (DEPTH, D_MODEL, 3 * D_MODEL), f32) * D_MODEL ** -0.5,
        "ada_b": nrm(ks[3], (DEPTH, 3 * D_MODEL), f32) * 0.02,
        "norm_g": 1.0 + 0.05 * nrm(ks[4], (DEPTH, D_MODEL), f32),
        "gla_w_in": nrm(ks[5], (n_gla, D_MODEL, GLA_IN), f32) * D_MODEL ** -0.5,
        "gla_w_alpha": nrm(ks[6], (n_gla, GLA_RANK, GLA_DK), f32) * GLA_RANK ** -0.5,
        "gla_b_alpha": nrm(ks[7], (n_gla, GLA_DK), f32) * 0.1,
        "gla_norm_g": 1.0 + 0.05 * nrm(ks[8], (n_gla, GLA_DV_HEAD), f32),
        "gla_w_out": nrm(ks[9], (n_gla, GLA_DV, D_MODEL), f32) * GLA_DV ** -0.5,
        "dsw_w_in": nrm(ks[10], (n_dsw, D_MODEL, DSW_IN), f32) * D_MODEL ** -0.5,
        "dsw_w_out": nrm(ks[11], (n_dsw, DSW_WIDTH, D_MODEL), f32) * DSW_WIDTH ** -0.5,
        "rel_bias": nrm(ks[12], (REL_BUCKETS, DSW_HEADS), f32) * 0.5,
        "final_g": 1.0 + 0.05 * nrm(ks[13], (D_MODEL,), f32),
    }


def reference(x, c, ada_w, ada_b, norm_g, gla_w_in, gla_w_alpha, gla_b_alpha, gla_norm_g,
              gla_w_out, dsw_w_in, dsw_w_out, rel_bias, final_g):
    c_act = jax.nn.silu(c)
    for i in range(DEPTH):
        mod = c_act @ ada_w[i] + ada_b[i]
        shift, scale, gate = jnp.split(mod, 3, axis=-1)
        h = rmsnorm(x, norm_g[i]) * (1.0 + scale[:, None]) + shift[:, None]
        j = i // N_MIXERS
        if i % N_MIXERS == 0:
            y = gla_mixer(h, gla_w_in[j], gla_w_alpha[j], gla_b_alpha[j], gla_norm_g[j], gla_w_out[j])
        else:
            y = dilated_mixer(h, dsw_w_in[j], dsw_w_out[j], rel_bias)
        x = x + gate[:, None] * y
    return rmsnorm(x, final_g)
```

```python
import math
from contextlib import ExitStack

import numpy as np
import concourse.bass as bass
import concourse.mybir as mybir
from concourse.bass_utils import run_bass_kernel_spmd

F32 = mybir.dt.float32
BF16 = mybir.dt.bfloat16
AF = mybir.ActivationFunctionType
ALU = mybir.AluOpType

S = 2048
D = 1024
NB = 2
EPS = 1e-6
NEG = -30000.0


class Res:
    __slots__ = ("name", "w", "r", "sem", "cnt")

    def __init__(self, name):
        self.name = name
        self.w = None
        self.r = []
        self.sem = None
        self.cnt = 0


class Op:
    __slots__ = ("eng", "fn", "deps", "kind", "sig", "val", "res", "ndma")


class Prog:
    ENG = ("sync", "act", "pool", "dve", "pe")

    def __init__(self):
        self.ops = []
        self.dma_res = []

    def add(self, eng, fn, reads=(), writes=(), dma_res=None, ndma=0):
        i = len(self.ops)
        op = Op()
        op.eng, op.fn, op.kind = eng, fn, ("d" if dma_res is not None else "c")
        op.sig, op.val, op.res, op.ndma = False, 0, dma_res, ndma
        deps = {}
        for r in reads:
            if r.w is not None:
                deps[r.w] = "raw"
        for w in writes:
            if w.w is not None and w.w not in deps:
                deps[w.w] = "waw"
            for j in w.r:
                if j not in deps:
                    deps[j] = "war"
        for j in [j for j in deps if self.ops[j].fn is None]:
            typ = deps.pop(j)
            if self.ops[j].eng == eng:
                continue
            for j2 in self.ops[j].deps:
                if j2 not in deps:
                    deps[j2] = "raw"
        out = []
        latest = {}
        for j, typ in deps.items():
            oj = self.ops[j]
            if oj.kind == "c" and oj.eng == eng and op.kind == "c":
                if eng == "pe":
                    continue
                if typ == "waw" and eng != "pool":
                    continue
            if oj.kind == "c":
                if latest.get(oj.eng, -1) < j:
                    latest[oj.eng] = j
            else:
                out.append(j)
        out.extend(latest.values())
        op.deps = out
        for r in reads:
            r.r.append(i)
        for w in writes:
            w.w = i
            w.r = []
        if dma_res is not None and dma_res not in self.dma_res:
            self.dma_res.append(dma_res)
        self.ops.append(op)
        return i

    def barrier(self):
        last = {}
        dmas = []
        for idx, op in enumerate(self.ops):
            if op.fn is None:
                continue
            if op.kind == "c":
                last[op.eng] = idx
            elif idx >= getattr(self, "bar_start", 0):
                dmas.append(idx)
        for eng in self.ENG:
            i = self.add(eng, None)
            self.ops[i].deps = [j for e2, j in last.items() if (e2 != eng or eng == "pool")] + list(dmas)
        self.bar_start = len(self.ops)

    def emit(self, nc, stack):
        ops = self.ops
        for op in ops:
            for j in op.deps:
                ops[j].sig = True
        esem = {e: stack.enter_context(nc.semaphore("sem_" + e)) for e in self.ENG}
        for r in self.dma_res:
            r.sem = stack.enter_context(nc.semaphore("dsem_" + r.name))
            r.cnt = 0
        ecnt = {e: 0 for e in self.ENG}
        for op in ops:
            if op.kind == "d":
                op.res.cnt += 16 * op.ndma
                op.val = op.res.cnt
            elif op.sig:
                ecnt[op.eng] += 1
                op.val = ecnt[op.eng]
        self.counts = dict(ecnt)
        by_eng = {e: [op for op in ops if op.eng == e] for e in self.ENG}

        def run(ename, e):
            waited = {}
            for op in by_eng[ename]:
                for j in op.deps:
                    oj = ops[j]
                    sem = oj.res.sem if oj.kind == "d" else esem[oj.eng]
                    key = sem.name
                    if waited.get(key, 0) >= oj.val:
                        continue
                    waited[key] = oj.val
                    e.wait_ge(sem, oj.val)
                if op.fn is None:
                    continue
                r = op.fn(e)
                if op.kind == "d":
                    assert len(r) == op.ndma
                    for ins in r:
                        ins.then_inc(op.res.sem, 16)
                elif op.sig:
                    r.then_inc(esem[op.eng], 1)

        block = stack.enter_context(nc.Block())

        @block.sync
        def _(e):
            run("sync", e)

        @block.scalar
        def _(e):
            run("act", e)

        @block.gpsimd
        def _(e):
            run("pool", e)

        @block.vector
        def _(e):
            run("dve", e)

        @block.tensor
        def _(e):
            run("pe", e)


def bc(ap, axis, n):
    a = [list(x) for x in ap.ap]
    a.insert(axis, [0, n])
    return bass.AP(ap.tensor, ap.offset, a)


def t5_bucket(n):
    max_exact = 16
    nf = np.maximum(n, 1).astype(np.float32)
    large = max_exact + (np.log(nf / max_exact) / math.log(2048 / max_exact) * (32 - max_exact)).astype(np.int32)
    large = np.minimum(large, 31)
    return np.where(n < max_exact, n, large).astype(np.int32)


DILS = (1, 4, 16)


def bias_tables(rel_bias):
    k = np.arange(128)[:, None]
    q = np.arange(128)[None, :]
    out = np.full((3, 16, 128, 2, 128), NEG, np.float32)
    for g, dil in enumerate(DILS):
        st = q + 128 - k
        valid = st <= 128
        b = t5_bucket(np.clip(st, 0, 128) * dil)
        for h in range(16):
            out[g, h, :, 0, :] = np.where(valid, rel_bias[b, h], NEG)
        st = q - k
        valid = st >= 0
        b = t5_bucket(np.clip(st, 0, 128) * dil)
        for h in range(16):
            out[g, h, :, 1, :] = np.where(valid, rel_bias[b, h], NEG)
    out = out.reshape(3, 8, 2, 128, 2, 128).transpose(1, 3, 0, 2, 4, 5)
    return np.ascontiguousarray(out)


def build(layers=(0, 1), final=True, dbg=False):
    nc = bass.Bass("TRN2", target_bir_lowering=False)
    st = ExitStack()
    P = Prog()

    def dram(name, shape, kind="ExternalInput", dt=F32):
        return nc.dram_tensor(name, list(shape), dt, kind=kind).ap()

    def sb(name, shape, dt):
        return st.enter_context(nc.sbuf_tensor(name, list(shape), dt))

    xT_d = dram("xT", [NB, 128, 8, S])
    cT_d = dram("cT", [128, 8, NB])
    adaw_d = dram("ada_w", [2, 128, 8, 3072])
    adab_d = dram("ada_bT", [128, 2, 24])
    ng_d = dram("norm_gT", [128, 2, 8])
    fg_d = dram("final_gT", [128, 8])
    cst_d = dram("consts", [128, 512])
    if 0 in layers:
        gw_d = dram("gla_wh", [4, 3, 128, 8, 256])
        gwl_d = dram("gla_wlr", [128, 8, 16])
        gwa_d = dram("gla_walpha", [17, 512])
        ggn_d = dram("gla_gnT", [128, 2])
        gwo_d = dram("gla_wo", [128, 8, 1024])
    if 1 in layers:
        dw_d = dram("dsw_wh", [8, 3, 128, 8, 384])
        dwg_d = dram("dsw_wg", [8, 128, 8, 128])
        dwo_d = dram("dsw_wo", [128, 8, 1024])
        db_d = dram("dsw_bias", [8, 128, 1536])
    outT_d = dram("outT", [NB, 128, 8, S], kind="ExternalOutput")

    xT = sb("xT_sb", [128, 8, S], F32)
    hT = sb("hT_sb", [128, 8, S], BF16)
    ogT = sb("ogT_sb", [128, 8, S], BF16)
    wbuf = sb("wbuf", [128, 8 * 1280], BF16)
    cst = sb("cst", [128, 512], F32)
    cstb = sb("cstb", [128, 512], BF16)
    c_sb = sb("c_sb", [128, 8, NB], F32)
    cact = sb("cact", [128, 8, NB], F32)
    adab = sb("adab", [128, 2, 24], F32)
    ngT = sb("ngT", [128, 2, 8], F32)
    fgT = sb("fgT", [128, 8], F32)
    modT = sb("modT", [128, 2, 24, NB], F32)
    gsT = sb("gsT", [128, 2, 8, NB], F32)
    ones_f = sb("ones_f", [128, 128], BF16)
    ones_hb = sb("ones_hb", [128, 128], BF16)

    ARENA16 = 24832 + 3072 - 1792 - 64 + 2048
    arena = sb("arena", [128, ARENA16], BF16)
    aoff = [0]

    def carve(shape, dt):
        n = 1
        for d_ in shape[1:]:
            n *= d_
        n16 = n * (2 if dt == F32 else 1)
        n16 = (n16 + 15) // 16 * 16
        assert aoff[0] + n16 <= ARENA16, (aoff[0], n16)
        v = arena[:, aoff[0]:aoff[0] + n16]
        aoff[0] += n16
        if dt == F32:
            v = v.bitcast(F32)
        v = v[:, 0:n]
        if len(shape) == 3:
            v = v.rearrange("p (a b) -> p a b", a=shape[1])
        return v

    R = {}

    def res(name):
        if name not in R:
            R[name] = Res(name)
        return R[name]

    wg = [res("wg%d" % i) for i in range(20)]
    xr2 = [[res("x%d_%d" % (t_, c)) for c in range(8)] for t_ in range(4)]
    xr = [r_ for t_ in xr2 for r_ in t_]
    hr = [res("h%d" % c) for c in range(8)]
    ogr = [res("og%d" % c) for c in range(8)]

    ps = [st.enter_context(nc.psum_tensor("ps%d" % i, [128, 512], F32)) for i in range(8)]
    pr = [res("psum%d" % i) for i in range(8)]
    acc_rot = [0]

    def acc_bank():
        acc_rot[0] ^= 1
        return acc_rot[0]

    ident = cstb[:, 0:128]
    ones64 = cstb[:, 384:448]
    U2 = cst[:, 128:256]
    maskT = cst[:, 256:384]

    P.add("sync", lambda e: [e.dma_start(out=cst[:], in_=cst_d),
                             e.dma_start(out=c_sb[:], in_=cT_d),
                             e.dma_start(out=adab[:], in_=adab_d),
                             e.dma_start(out=ngT[:], in_=ng_d),
                             e.dma_start(out=fgT[:], in_=fg_d)],
          writes=[res("cst"), res("smallin")], dma_res=res("cst"), ndma=5)
    P.add("dve", lambda e: e.tensor_copy(out=cstb[:], in_=cst[:]), reads=[res("cst")], writes=[res("cstb")])
    P.add("pool", lambda e: e.memset(ones_f[:], 1.0 / 1024.0), writes=[res("ones_f")])
    P.add("pool", lambda e: e.memset(ones_hb[:], 1.0 / 256.0), writes=[res("cstb")])
    P.add("act", lambda e: e.activation(out=cact[:], in_=c_sb[:], func=AF.Silu),
          reads=[res("cst")], writes=[res("cact")])

    ogT_f = ogT[:].rearrange("p c t -> p (c t)").bitcast(F32)
    mrow = arena[0:2, 20480:22528].bitcast(F32).rearrange("p (s n) -> p s n", s=2)
    def mod_piece(l, piece):
        pm = ps[3]
        half = piece % 2
        stg = ogT_f[:, half * 4096:(half + 1) * 4096].rearrange("p (k n) -> p k n", k=8)
        sres = ogr[half * 4:(half + 1) * 4]
        P.add("sync", lambda e, stg=stg, l=l, piece=piece: [
            e.dma_start(out=stg, in_=adaw_d[l, :, :, piece * 512:(piece + 1) * 512])],
            writes=sres, dma_res=res("adaw%d" % half), ndma=1)
        for k in range(8):
            P.add("pe", lambda e, stg=stg, k=k: e.matmul(
                ps[2][0:NB, :], lhsT=cact[:, k, :], rhs=stg[:, k, :], start=(k == 0), stop=(k == 7)),
                reads=sres + [res("cact")], writes=[pr[2]])
        P.add("act", lambda e, half=half: e.activation(out=mrow[:, half, :], in_=ps[2][0:NB, :], func=AF.Identity),
              reads=[pr[2]], writes=[res("mrow%d" % half)])
        for fi in range(4):
            f = piece * 4 + fi
            P.add("pe", lambda e, f=f, fi=fi, half=half, pm=pm: e.matmul(
                pm[:, f * NB:(f + 1) * NB], lhsT=mrow[:, half, fi * 128:(fi + 1) * 128], rhs=cst[0:NB, 0:NB],
                start=True, stop=True),
                reads=[res("mrow%d" % half), res("cst")], writes=[pr[3]])

    def mod_finish(l):
        pm = ps[3]
        P.add("dve", lambda e, l=l, pm=pm: e.tensor_tensor(
            out=modT[:, l, :, :], in0=pm[:, 0:24 * NB].rearrange("p (f b) -> p f b", b=NB),
            in1=bc(adab[:, l, :], 2, NB), op=ALU.add),
            reads=[pr[3], res("smallin")], writes=[res("mod%d" % l)])
        P.add("dve", lambda e, l=l: e.tensor_scalar(
            out=gsT[:, l, :, :], in0=modT[:, l, 8:16, :], scalar1=1.0, scalar2=None, op0=ALU.add),
            reads=[res("mod%d" % l)], writes=[res("gs%d" % l)])
        P.add("dve", lambda e, l=l: e.tensor_tensor(
            out=gsT[:, l, :, :], in0=gsT[:, l, :, :], in1=bc(ngT[:, l, :], 2, NB), op=ALU.mult),
            reads=[res("gs%d" % l), res("smallin")], writes=[res("gs%d" % l)])

    for piece in range(6):
        mod_piece(layers[0], piece)
    mod_finish(layers[0])
    deferred_mod = [(layers[1], p) for p in range(6)] if len(layers) > 1 else []

    st_ops = []
    cur_alias = {}

    def sr(name):
        return [res(name)]

    hp_sq = [arena[:, i * 4096:i * 4096 + 2048].rearrange("p (c t) -> p c t", c=4) for i in range(2)]
    hp_t = [arena[:, 8192 + i * 4096:8192 + (i + 1) * 4096].bitcast(F32).rearrange("p (c t) -> p c t", c=4) for i in range(2)]
    hp_rstd = [arena[:, 16384 + i * 1024:16384 + (i + 1) * 1024].bitcast(F32) for i in range(2)]

    def stats(tile):
        tsl = slice(tile * 512, (tile + 1) * 512)
        rs = hp_rstd[tile % 2]
        rname = "hp_rstd%d" % (tile % 2)
        for hf in range(2):
            P.add("act", lambda e, hf=hf: e.activation(out=hp_sq[hf], in_=xT[:, 4 * hf:4 * hf + 4, tsl], func=AF.Square),
                  reads=xr2[tile][4 * hf:4 * hf + 4], writes=[res("hp_sq%d" % hf)])
            for ci in range(4):
                c = 4 * hf + ci
                P.add("pe", lambda e, hf=hf, ci=ci, c=c: e.matmul(ps[2][:, :], lhsT=ones_f[:], rhs=hp_sq[hf][:, ci, :],
                                                                start=(c == 0), stop=(c == 7)),
                      reads=[res("hp_sq%d" % hf), res("ones_f")], writes=[pr[2]])
        P.add("act", lambda e, rs=rs: e.activation(out=rs, in_=ps[2][:, :], func=AF.Ln, bias=EPS),
              reads=[pr[2]], writes=[res(rname)])
        P.add("act", lambda e, rs=rs: e.activation(out=rs, in_=rs, func=AF.Exp, scale=-0.5),
              reads=[res(rname)], writes=[res(rname)])
        return rs, res(rname)

    def apply_h(l, b, tile, rs, rres):
            tsl = slice(tile * 512, (tile + 1) * 512)
            for hf in range(2):
                P.add("dve", lambda e, hf=hf, tsl=tsl, rs=rs: e.tensor_tensor(
                    out=hp_t[hf], in0=xT[:, 4 * hf:4 * hf + 4, tsl], in1=bc(rs, 1, 4), op=ALU.mult),
                    reads=xr2[tile][4 * hf:4 * hf + 4] + [rres], writes=sr("hp_t%d" % hf))
                for ci in range(4):
                    c = 4 * hf + ci
                    if ci % 2 == 0:
                        P.add("act", lambda e, c=c, ci=ci, hf=hf, tsl=tsl: e.activation(
                            out=hT[:, c, tsl], in_=hp_t[hf][:, ci, :], func=AF.Identity,
                            bias=modT[:, l, c, b:b + 1], scale=gsT[:, l, c, b:b + 1]),
                            reads=sr("hp_t%d" % hf) + [res("mod%d" % l), res("gs%d" % l)], writes=[hr[c]])
                    else:
                        P.add("dve", lambda e, c=c, ci=ci, hf=hf, tsl=tsl: e.tensor_scalar(
                            out=hT[:, c, tsl], in0=hp_t[hf][:, ci, :], scalar1=gsT[:, l, c, b:b + 1],
                            scalar2=modT[:, l, c, b:b + 1], op0=ALU.mult, op1=ALU.add),
                            reads=sr("hp_t%d" % hf) + [res("mod%d" % l), res("gs%d" % l)], writes=[hr[c]])

    def hprep(l, b):
        if b == 0 or 1 not in layers:
            P.barrier()
        else:
            P.add("act", None, writes=[res("hp_sq0"), res("accN")])
            P.add("act", None, writes=[res("hp_sq1"), res("accD")] + [res("fs_sq%d" % i_) for i_ in range(5)])
            P.add("dve", None, writes=[res("hp_t0"), res("qT1"), res("kT1")])
            P.add("dve", None, writes=[res("hp_t1"), res("kT1"), res("vT1")])
            P.add("act", None, writes=[res("hp_rstd0"), res("hp_rstd1"), res("vtm1")])
        if deferred_mod:
            mod_piece(*deferred_mod.pop(0))
        nxt_stats = stats(0)
        for tile in range(4):
            rs, rres = nxt_stats
            if tile + 1 < 4:
                if deferred_mod:
                    mod_piece(*deferred_mod.pop(0))
                nxt_stats = stats(tile + 1)
            apply_h(l, b, tile, rs, rres)
        if deferred_mod:
            l2 = deferred_mod[0][0]
            while deferred_mod:
                mod_piece(*deferred_mod.pop(0))
            mod_finish(l2)

    fs_sq = [arena[:, 4096 + i * 512:4096 + (i + 1) * 512] for i in range(5)]
    hT_f = hT[:].rearrange("p c t -> p (c t)").bitcast(F32)

    def apply_final(b, tile, rs, rres, stg, sres):
        tsl = slice(tile * 512, (tile + 1) * 512)
        for hf in range(2):
            P.add("dve", lambda e, hf=hf, tsl=tsl, rs=rs: e.tensor_tensor(
                out=hp_t[hf], in0=xT[:, 4 * hf:4 * hf + 4, tsl], in1=bc(rs, 1, 4), op=ALU.mult),
                reads=xr2[tile][4 * hf:4 * hf + 4] + [rres], writes=sr("hp_t%d" % hf))
            for ci in range(4):
                c = 4 * hf + ci
                if ci % 2 == 0:
                    P.add("act", lambda e, c=c, ci=ci, hf=hf, stg=stg: e.activation(
                        out=stg[:, c, :], in_=hp_t[hf][:, ci, :], func=AF.Identity, scale=fgT[:, c:c + 1]),
                        reads=sr("hp_t%d" % hf) + [res("smallin")], writes=sres)
                else:
                    P.add("dve", lambda e, c=c, ci=ci, hf=hf, stg=stg: e.tensor_scalar(
                        out=stg[:, c, :], in0=hp_t[hf][:, ci, :], scalar1=fgT[:, c:c + 1], scalar2=None, op0=ALU.mult),
                        reads=sr("hp_t%d" % hf) + [res("smallin")], writes=sres)
        st_ops.append(P.add("sync", lambda e, tsl=tsl, stg=stg: [e.dma_start(out=outT_d[b, :, :, tsl], in_=stg)],
                            reads=sres, dma_res=res("outst%d" % (tile % 2)), ndma=1))

    wo_view = wbuf[:, 0:8192].rearrange("p (e n) -> p e n", e=8)

    def load_wout(w_d, l, chunks):
        gr = []
        for e_ in chunks:
            gr += wg[2 * e_:2 * e_ + 2]
        P.add("pool", lambda e, chunks=tuple(chunks): [e.dma_start(out=wo_view[:, e_, :], in_=w_d[:, e_, :]) for e_ in chunks],
              writes=gr, dma_res=res("wo%d_%d" % (l, len(chunks))), ndma=len(chunks))
        return wo_view

    def outproj(l, b, wo, nxt=None, alias=None):
        LAG = 4
        cur_alias.clear()
        if nxt is not None:
            if alias is None:
                P.barrier()
            else:
                for eng_, names in (("act", ["fs_sq0", "fs_sq1", "fs_sq2", "fs_sq3", "fs_sq4", "hp_rstd0", "hp_rstd1"]),
                                    ("dve", ["hp_t0", "hp_t1"])):
                    for nm in names:
                        al = [x for pre, lst in alias.items() if nm.startswith(pre) for x in lst]
                        P.add(eng_, None, writes=[res(nm)] + [res(x) for x in al])
        for tile in range(4):
            tsl = slice(tile * 512, (tile + 1) * 512)

            def stat_mm(n):
                P.add("pe", lambda e, n=n: e.matmul(ps[2][:, :], lhsT=ones_f[:], rhs=fs_sq[n % 5],
                                                    start=(n == 0), stop=(n == 7)),
                      reads=sr("fs_sq%d" % (n % 5)) + [res("ones_f")], writes=[pr[2]])

            for n in range(8):
                bk = acc_bank()
                for e_ in range(8):
                    P.add("pe", lambda e, e_=e_, n=n, bk=bk, tsl=tsl: e.matmul(
                        ps[bk][:, :], lhsT=wo[:, e_, n * 128:(n + 1) * 128], rhs=ogT[:, e_, tsl],
                        start=(e_ == 0), stop=(e_ == 7)),
                        reads=wg[0:16] + [ogr[e_]], writes=[pr[bk]])
                P.add("dve", lambda e, n=n, bk=bk, tsl=tsl: e.scalar_tensor_tensor(
                    out=xT[:, n, tsl], in0=ps[bk][:, :], scalar=modT[:, l, 16 + n, b:b + 1], in1=xT[:, n, tsl],
                    op0=ALU.mult, op1=ALU.add),
                    reads=[pr[bk], xr2[tile][n], res("mod%d" % l)], writes=[xr2[tile][n]])
                if nxt is not None:
                    P.add("act", lambda e, n=n, tsl=tsl: e.activation(out=fs_sq[n % 5], in_=xT[:, n, tsl], func=AF.Square),
                          reads=[xr2[tile][n]], writes=sr("fs_sq%d" % (n % 5)))
                    if n >= LAG:
                        stat_mm(n - LAG)
            if nxt is not None:
                for n in range(8 - LAG, 8):
                    stat_mm(n)
                rs = hp_rstd[tile % 2]
                rname = "hp_rstd%d" % (tile % 2)
                P.add("act", lambda e, rs=rs: e.activation(out=rs, in_=ps[2][:, :], func=AF.Ln, bias=EPS),
                      reads=[pr[2]], writes=sr(rname))
                P.add("act", lambda e, rs=rs: e.activation(out=rs, in_=rs, func=AF.Exp, scale=-0.5),
                      reads=sr(rname), writes=sr(rname))
                if nxt[0] == "hprep":
                    apply_h(nxt[1], b, tile, rs, res(rname))
                else:
                    half = tile % 2
                    stg = hT_f[:, half * 4096:(half + 1) * 4096].rearrange("p (k n) -> p k n", k=8)
                    apply_final(b, tile, rs, res(rname), stg, hr[half * 4:(half + 1) * 4])
        cur_alias.clear()

    def proj_fm(wk, col0, tile, bk, wres):
        tsl = slice(tile * 512, (tile + 1) * 512)
        for k in range(8):
            P.add("pe", lambda e, k=k: e.matmul(ps[bk][:, :], lhsT=wk[:, k, col0:col0 + 128], rhs=hT[:, k, tsl],
                                                start=(k == 0), stop=(k == 7)),
                  reads=wres + [hr[k]], writes=[pr[bk]])

    if 0 in layers:
        aoff[0] = 0
        ez = carve([128, 512], F32)
        lap = carve([128, 4, 128], F32)
        E1 = carve([128, 512], F32)
        E2 = carve([128, 512], F32)
        wst0 = arena[:, 0:4096].bitcast(F32).rearrange("p (s n) -> p s n", s=2)
        E3 = ez
        glrT = carve([128, 512], F32)
        qTs = carve([128, S], BF16)
        kTs = carve([128, S], BF16)
        kkvT = carve([128, S], BF16)
        vtm = carve([128, 16, 256], BF16)
        ktmA = carve([128, 16, 128], BF16)
        ktmB = carve([128, 16, 128], BF16)
        ract = carve([128, 4, 256], BF16)
        dec = carve([128, 32], F32)
        ATs = carve([128, 4, 128], BF16)
        Sall = arena[:, 1024:3584].bitcast(F32).rearrange("p (s e) -> p s e", s=5)
        Sbf = carve([128, 8, 256], BF16)
        osq = ez[:, 0:256].bitcast(BF16)
        rso = carve([128, 2, 128], F32)
        otmp = carve([128, 1024], F32)
        walp = carve([128, 512], F32)
        wlr = carve([128, 8, 16], BF16)
        gnT = carve([128, 2], F32)

    def gla_init():
        P.barrier()
        P.add("pool", lambda e: [e.dma_start(out=wlr, in_=gwl_d)], writes=[res("wlr")], dma_res=res("wlr"), ndma=1)
        P.add("sync", lambda e: [e.dma_start(out=walp[0:17, :], in_=gwa_d), e.dma_start(out=gnT, in_=ggn_d)],
              writes=[res("walp"), res("gnT")], dma_res=res("walp"), ndma=2)
        P.add("pool", lambda e: e.memset(glrT[0:32, :], 1.0), writes=[res("glrT")])
        P.add("pool", lambda e: e.memset(ktmA[:], 0.0), writes=[res("ktm")])
        P.add("pool", lambda e: e.memset(ktmB[:], 0.0), writes=[res("ktm")])

    def gla_layer(b):
        l = 0
        hprep(l, b)
        gla_init()
        def slot0(h, part):
            i = (3 * h + part) % 5
            return wbuf[:, i * 2048:(i + 1) * 2048].rearrange("p (k n) -> p k n", k=8), wg[4 * i:4 * i + 4], i

        def wdma0(h, part):
            ap, rs, i = slot0(h, part)
            P.add("pool", lambda e, ap=ap, h=h, part=part: [e.dma_start(out=ap, in_=gw_d[h, part])],
                  writes=rs, dma_res=res("w0slot%d" % i), ndma=1)

        for part in range(3):
            wdma0(0, part)
        for h in range(4):
            if h + 1 < 4:
                wdma0(h + 1, 0)
                wdma0(h + 1, 1)
            wA, rA, _ = slot0(h, 0)
            wV, rV, _ = slot0(h, 1)
            wR, rR, _ = slot0(h, 2)
            def emit_glr(tile):
                tsl = slice(tile * 512, (tile + 1) * 512)
                bk = acc_bank()
                for k in range(8):
                    P.add("pe", lambda e, k=k, bk=bk, tsl=tsl: e.matmul(
                        ps[bk][0:16, :], lhsT=wlr[:, k, :], rhs=hT[:, k, tsl], start=(k == 0), stop=(k == 7)),
                        reads=[res("wlr"), hr[k]], writes=[pr[bk]])
                P.add("dve", lambda e, bk=bk: e.tensor_copy(out=glrT[0:16, :], in_=ps[bk][0:16, :]),
                      reads=[pr[bk]], writes=[res("glrT")])

            def emit_z(tile, h=h):
                for s4 in range(4):
                    P.add("pe", lambda e, s4=s4, h=h: e.matmul(
                        ps[3][:, s4 * 128:(s4 + 1) * 128], lhsT=glrT[0:17, s4 * 128:(s4 + 1) * 128],
                        rhs=walp[0:17, h * 128:(h + 1) * 128], start=True, stop=True),
                        reads=[res("glrT"), res("walp")], writes=[pr[3]])
                P.add("act", lambda e: e.activation(out=ez[:], in_=ps[3][:, :], func=AF.Exp, scale=-1.0),
                      reads=[pr[3]], writes=[res("ez")])
                P.add("act", lambda e: e.activation(out=lap[:].rearrange("p s d -> p (s d)"), in_=ez[:], func=AF.Ln, bias=1.0),
                      reads=[res("ez")], writes=[res("lap")])

            def emit_bT(tile):
                for s4 in range(4):
                    P.add("pe", lambda e, s4=s4: e.matmul(
                        ps[4][:, s4 * 128:(s4 + 1) * 128], lhsT=lap[:, s4, :], rhs=U2, start=True, stop=True),
                        reads=[res("lap"), res("cst")], writes=[pr[4]])
                P.add("act", lambda e: e.activation(out=E1[:], in_=ps[4][:, :], func=AF.Exp, bias=math.log(128 ** -0.5)),
                      reads=[pr[4]], writes=[res("E1")])
                P.add("act", lambda e: e.activation(out=E2[:], in_=ps[4][:, :], func=AF.Exp, scale=-1.0),
                      reads=[pr[4]], writes=[res("E2")])
                P.add("act", lambda e, tile=tile: e.activation(
                    out=dec[:, tile * 8:(tile + 1) * 8],
                    in_=ps[4][:, :].rearrange("p (c j) -> p c j", j=64)[:, :, 63], func=AF.Exp),
                    reads=[pr[4]], writes=[res("dec")])
                P.add("dve", lambda e, tile=tile: e.tensor_tensor(
                    out=E3[:, :].rearrange("p (c j) -> p c j", j=64), in0=E2[:].rearrange("p (c j) -> p c j", j=64),
                    in1=bc(dec[:, tile * 8:(tile + 1) * 8], 2, 64), op=ALU.mult),
                    reads=[res("E2"), res("dec")], writes=[res("ez")])

            def emit_qk(tile, wA=wA, rA=rA):
                tsl = slice(tile * 512, (tile + 1) * 512)
                bk = acc_bank()
                proj_fm(wA, 0, tile, bk, rA)
                P.add("dve", lambda e, bk=bk, tsl=tsl: e.tensor_tensor(out=qTs[:, tsl], in0=ps[bk][:, :], in1=E1[:], op=ALU.mult),
                      reads=[pr[bk], res("E1")], writes=[res("qTs")])
                bk = acc_bank()
                proj_fm(wA, 128, tile, bk, rA)
                P.add("dve", lambda e, bk=bk, tsl=tsl: e.tensor_tensor(out=kTs[:, tsl], in0=ps[bk][:, :], in1=E2[:], op=ALU.mult),
                      reads=[pr[bk], res("E2")], writes=[res("kTs")])
                P.add("dve", lambda e, bk=bk, tsl=tsl: e.tensor_tensor(out=kkvT[:, tsl], in0=ps[bk][:, :], in1=E3[:], op=ALU.mult),
                      reads=[pr[bk], res("ez")], writes=[res("kkvT")])

            def emit_vhalf(tile, sp, wV=wV, rV=rV):
                bk = acc_bank()
                for s2 in range(2):
                    s_ = tile * 4 + sp * 2 + s2
                    for k in range(8):
                        P.add("pe", lambda e, k=k, s_=s_, s2=s2, bk=bk, wV=wV: e.matmul(
                            ps[bk][:, s2 * 256:(s2 + 1) * 256], lhsT=hT[:, k, s_ * 128:(s_ + 1) * 128],
                            rhs=wV[:, k, 0:256], start=(k == 0), stop=(k == 7)),
                            reads=rV + [hr[k]], writes=[pr[bk]])
                s0 = tile * 4 + sp * 2
                P.add("act", lambda e, bk=bk, s0=s0: e.activation(
                    out=vtm[:, s0:s0 + 2, :].rearrange("p s d -> p (s d)"), in_=ps[bk][:, :], func=AF.Identity),
                    reads=[pr[bk]], writes=[res("vtm")])

            def emit_tr(tile):
                pv = ps[5][:, :].bitcast(BF16)
                for s4 in range(4):
                    s_ = tile * 4 + s4
                    P.add("pe", lambda e, s_=s_, s4=s4, pv=pv: e.transpose(
                        pv[:, s4 * 128:(s4 + 1) * 128], kkvT[:, s_ * 128:(s_ + 1) * 128], ident),
                        reads=[res("kkvT"), res("cstb")], writes=[pr[5]])
                s0 = tile * 4
                P.add("dve", lambda e, pv=pv, s0=s0: e.tensor_copy(
                    out=ktmA[0:64, s0:s0 + 4, :], in_=pv[0:64, 0:512].rearrange("p (s d) -> p s d", s=4)),
                    reads=[pr[5]], writes=[res("ktm")])
                P.add("dve", lambda e, pv=pv, s0=s0: e.tensor_copy(
                    out=ktmB[64:128, s0:s0 + 4, :], in_=pv[64:128, 0:512].rearrange("p (s d) -> p s d", s=4)),
                    reads=[pr[5]], writes=[res("ktm")])

            for tile in range(4):
                emit_glr(tile)
                if tile > 0:
                    emit_tr(tile - 1)
                emit_vhalf(tile, 0)
                emit_z(tile)
                emit_vhalf(tile, 1)
                emit_bT(tile)
                emit_qk(tile)
            emit_tr(3)
            if h + 1 < 4:
                wdma0(h + 1, 2)
            else:
                load_wout(gwo_d, 0, [0, 1, 4, 5, 6, 7])
            SN = [res("lap"), res("E1"), res("E2")]
            P.add("pool", lambda e: e.memset(Sall[:, 0, :], 0.0), writes=SN)

            def emit_A(j):
                for sub in range(2):
                    s_ = 2 * j + sub
                    ssl = slice(s_ * 128, (s_ + 1) * 128)
                    P.add("pe", lambda e, ssl=ssl, sub=sub: e.matmul(
                        ps[5][:, sub * 128:(sub + 1) * 128], lhsT=kTs[:, ssl], rhs=qTs[:, ssl], start=True, stop=True),
                        reads=[res("kTs"), res("qTs")], writes=[pr[5]])

            def emit_mask(j):
                ab = j % 2
                P.add("dve", lambda e, ab=ab: e.tensor_tensor(
                    out=ATs[:, ab * 2:ab * 2 + 2, :], in0=ps[5][:, 0:256].rearrange("p (s t) -> p s t", s=2),
                    in1=bc(maskT, 1, 2), op=ALU.mult),
                    reads=[pr[5], res("cst")], writes=[res("ATs%d" % ab)])

            def emit_kv(j):
                for i in range(4):
                    c = 4 * j + i
                    s_ = c // 2
                    kb = 7 if i < 2 else 3
                    ktm_c = ktmA if c % 2 == 0 else ktmB
                    P.add("pe", lambda e, ktm_c=ktm_c, s_=s_, kb=kb, i=i: e.matmul(
                        ps[kb][:, (i % 2) * 256:(i % 2 + 1) * 256], lhsT=ktm_c[:, s_, :], rhs=vtm[:, s_, :], start=True, stop=True),
                        reads=[res("ktm"), res("vtm")], writes=[pr[kb]])

            def emit_scan(j):
                for i in range(4):
                    c = 4 * j + i
                    kb = 7 if i < 2 else 3
                    si, so = (i, i + 1) if j % 2 == 0 else (4 - i, 3 - i)
                    P.add("dve", lambda e, c=c, kb=kb, i=i, si=si, so=so: e.scalar_tensor_tensor(
                        out=Sall[:, so, :], in0=Sall[:, si, :], scalar=dec[:, c:c + 1],
                        in1=ps[kb][:, (i % 2) * 256:(i % 2 + 1) * 256], op0=ALU.mult, op1=ALU.add),
                        reads=SN + [pr[kb], res("dec")], writes=SN)

            def emit_copies(j):
                lo = 0 if j % 2 == 0 else 1
                P.add("dve", lambda e, j=j, lo=lo: e.tensor_copy(
                    out=Sbf[:, (j % 2) * 4:(j % 2) * 4 + 4, :], in_=Sall[:, lo:lo + 4, :]),
                    reads=SN, writes=[res("Sbf%d" % (j % 2))])

            def emit_rproj(j):
                t0 = j * 256
                bk = acc_bank()
                for half in range(2):
                    for k in range(8):
                        P.add("pe", lambda e, k=k, half=half, bk=bk, t0=t0, wR=wR: e.matmul(
                            ps[bk][:, half * 256:(half + 1) * 256], lhsT=wR[:, k, half * 128:(half + 1) * 128],
                            rhs=hT[:, k, t0:t0 + 256], start=(k == 0), stop=(k == 7)),
                            reads=rR + [hr[k]], writes=[pr[bk]])
                rb = j % 2
                sgt = otmp[:, rb * 512:(rb + 1) * 512]
                P.add("act", lambda e, bk=bk, sgt=sgt: e.activation(out=sgt, in_=ps[bk][:, :], func=AF.Exp, scale=-1.0),
                      reads=[pr[bk]], writes=[res("otmp%d" % rb)])
                P.add("act", lambda e, sgt=sgt: e.activation(out=sgt, in_=sgt, func=AF.Ln, bias=1.0),
                      reads=[res("otmp%d" % rb)], writes=[res("otmp%d" % rb)])
                P.add("act", lambda e, sgt=sgt: e.activation(out=sgt, in_=sgt, func=AF.Exp, scale=-1.0),
                      reads=[res("otmp%d" % rb)], writes=[res("otmp%d" % rb)])
                for half in range(2):
                    P.add("dve", lambda e, bk=bk, rb=rb, sgt=sgt, half=half: e.scalar_tensor_tensor(
                        out=ract[:, 2 * rb + half, :], in0=ps[bk][:, half * 256:(half + 1) * 256], scalar=gnT[:, half:half + 1],
                        in1=sgt[:, half * 256:(half + 1) * 256], op0=ALU.mult, op1=ALU.mult),
                        reads=[pr[bk], res("otmp%d" % rb), res("gnT")], writes=[res("ract%d" % rb)])

            def emit_oT(j, j_=None):
                ab = j % 2
                ob = 6 if j % 2 == 0 else 4
                first = True
                for sub in range(2):
                    s_ = 2 * j + sub
                    for half in range(2):
                        ocol = (sub * 2 + half) * 128
                        P.add("pe", lambda e, half=half, ocol=ocol, ab=ab, sub=sub, s_=s_, first=first, ob=ob: e.matmul(
                            ps[ob][:, ocol:ocol + 128], lhsT=vtm[:, s_, half * 128:(half + 1) * 128],
                            rhs=ATs[:, ab * 2 + sub, :], start=first, stop=False, skip_group_check=True),
                            reads=[res("vtm"), res("ATs%d" % ab)], writes=[pr[ob]])
                        first = False
                        for cc in range(2):
                            c = 2 * s_ + cc
                            i = c - 4 * j
                            if c == 0:
                                continue
                            P.add("pe", lambda e, half=half, ocol=ocol, cc=cc, c=c, i=i, ob=ob: e.matmul(
                                ps[ob][:, ocol + cc * 64:ocol + (cc + 1) * 64], lhsT=Sbf[:, (j % 2) * 4 + (i if j % 2 == 0 else 3 - i), half * 128:(half + 1) * 128],
                                rhs=qTs[:, c * 64:(c + 1) * 64], start=False, stop=(cc == 1), skip_group_check=True),
                                reads=[res("Sbf%d" % (j % 2)), res("qTs")], writes=[pr[ob]])

            def emit_post1(j):
                ob = 6 if j % 2 == 0 else 4
                P.add("act", lambda e, ob=ob: e.activation(out=osq[:], in_=ps[ob][:, :], func=AF.Square),
                      reads=[pr[ob]], writes=[res("ez")])

            def emit_post2(j):
                t0 = j * 256
                ob = 6 if j % 2 == 0 else 4
                for sb_ in range(2):
                    for half in range(2):
                        col = (sb_ * 2 + half) * 128
                        P.add("pe", lambda e, sb_=sb_, half=half, col=col: e.matmul(
                            ps[2][:, sb_ * 128:(sb_ + 1) * 128], lhsT=ones_hb[:], rhs=osq[:, col:col + 128],
                            start=(half == 0), stop=(half == 1)),
                            reads=[res("ez"), res("cstb")], writes=[pr[2]])
                P.add("act", lambda e: e.activation(out=rso[:].rearrange("p s t -> p (s t)"), in_=ps[2][:, 0:256], func=AF.Ln, bias=EPS),
                      reads=[pr[2]], writes=[res("rso")])
                P.add("act", lambda e: e.activation(out=rso[:].rearrange("p s t -> p (s t)"), in_=rso[:].rearrange("p s t -> p (s t)"),
                                                    func=AF.Exp, scale=-0.5),
                      reads=[res("rso")], writes=[res("rso")])
                rb = j % 2
                ot = otmp[:, rb * 512:(rb + 1) * 512]
                P.add("dve", lambda e, ob=ob, ot=ot: e.tensor_tensor(
                    out=ot.rearrange("p (s h t) -> p s h t", s=2, h=2),
                    in0=ps[ob][:, :].rearrange("p (s h t) -> p s h t", s=2, h=2),
                    in1=bc(rso[:], 2, 2), op=ALU.mult),
                    reads=[pr[ob], res("rso")], writes=[res("otmp%d" % rb)])
                for half in range(2):
                    P.add("pool", lambda e, t0=t0, h=h, rb=rb, ot=ot, half=half: e.tensor_tensor(
                        out=ogT[:, 2 * h + half, t0:t0 + 256].rearrange("p (s t) -> p s t", s=2),
                        in0=ot.rearrange("p (s h t) -> p s h t", s=2, h=2)[:, :, half, :],
                        in1=ract[:, 2 * rb + half, :].rearrange("p (s t) -> p s t", s=2), op=ALU.mult),
                        reads=[res("otmp%d" % rb), res("ract%d" % rb)], writes=[ogr[2 * h + half]])

            emit_A(0)
            emit_kv(0)
            emit_scan(0)
            emit_copies(0)
            emit_mask(0)
            for j in range(8):
                if j + 1 < 8:
                    emit_A(j + 1)
                    emit_mask(j + 1)
                    emit_kv(j + 1)
                    emit_scan(j + 1)
                    emit_copies(j + 1)
                emit_rproj(j)
                if j > 0:
                    emit_post2(j - 1)
                emit_oT(j)
                emit_post1(j)
            emit_post2(7)
        wo = load_wout(gwo_d, 0, [2, 3])
        outproj(0, b, wo, nxt=(("hprep", 1) if 1 in layers else None),
                alias={"fs_sq": ["glrT", "qTs"], "hp_t": ["kTs", "kkvT", "vtm", "ktm"], "hp_rstd": ["ktm"]})

    if 1 in layers:
        aoff[0] = 0
        accN = carve([128, S], F32)
        wst1 = arena[:, 0:4096].bitcast(F32).rearrange("p (s n) -> p s n", s=2)
        accD = carve([128, S], F32)
        qT1 = carve([128, S], BF16)
        kTA = carve([128, S], BF16)
        kTB = carve([128, S], BF16)
        vT1 = carve([128, S], BF16)
        vtm1 = carve([128, 16, 128], BF16)
        sg = carve([128, S], BF16)
        bias8 = carve([128, 1536], BF16)
        PT = carve([128, 2, 512], BF16)

    def dsw_init():
        P.barrier()
        P.add("pool", lambda e: e.memset(kTA[:], 0.0), writes=[res("kT1")])
        P.add("pool", lambda e: e.memset(kTB[:], 0.0), writes=[res("kT1")])

    def perm_views(dst, tile, dil, psrc):
        if dil == 1:
            return dst[:, tile * 512:(tile + 1) * 512], psrc
        L = S // dil
        n_i = 512 // dil
        o = dst[:, :].rearrange("p (r i) -> p r i", r=dil)[:, :, tile * n_i:(tile + 1) * n_i]
        i_ = psrc.rearrange("p (i r) -> p r i", r=dil)
        return o, i_

    def dsw_layer(b):
        l = 1
        if 0 not in layers:
            hprep(l, b)
        dsw_init()
        bv = bias8[:, :].rearrange("p (g h m q) -> p g h m q", g=3, h=2, m=2)
        wgate = wbuf[:, 9216:10240].rearrange("p (k n) -> p k n", k=8)
        rgate = wg[18:20]

        def slot1(u):
            j = u % 3
            return wbuf[:, j * 3072:(j + 1) * 3072].rearrange("p (k n) -> p k n", k=8), wg[6 * j:6 * j + 6], j

        def wdma1(u):
            ap, rs, j = slot1(u)
            P.add("pool", lambda e, ap=ap, u=u: [e.dma_start(out=ap, in_=dw_d[u // 3, u % 3])],
                  writes=rs, dma_res=res("w1slot%d" % j), ndma=1)

        def gdma(hp):
            P.add("pool", lambda e, hp=hp: [e.dma_start(out=wgate, in_=dwg_d[hp])],
                  writes=rgate, dma_res=res("w1gate"), ndma=1)

        gdma(0)
        wdma1(0)
        wdma1(1)
        def emit_combine(hp, t):
            ts_ = slice(t * 512, (t + 1) * 512)
            P.add("act", lambda e, ts_=ts_: e.activation(out=accD[:, ts_], in_=accD[:, ts_], func=AF.Ln),
                  reads=[res("accD")], writes=[res("accD")])
            P.add("act", lambda e, ts_=ts_: e.activation(out=accD[:, ts_], in_=accD[:, ts_], func=AF.Exp, scale=-1.0),
                  reads=[res("accD")], writes=[res("accD")])
            P.add("dve", lambda e, ts_=ts_: e.tensor_tensor(out=accN[:, ts_], in0=accN[:, ts_], in1=accD[:, ts_], op=ALU.mult),
                  reads=[res("accN"), res("accD")], writes=[res("accN")])
            P.add("dve", lambda e, hp=hp, ts_=ts_: e.tensor_tensor(out=ogT[:, hp, ts_], in0=accN[:, ts_], in1=sg[:, ts_], op=ALU.mult),
                  reads=[res("accN"), res("sg")], writes=[ogr[hp]])

        nd_rot = [0]
        acc4_rot = [0]

        def acc4():
            acc4_rot[0] = (acc4_rot[0] + 1) % 4
            return (0, 1, 3, 4)[acc4_rot[0]]

        for hp in range(8):
            P.add("pool", lambda e, hp=hp: [e.dma_start(out=bias8, in_=db_d[hp])],
                  writes=[res("bias8")], dma_res=res("bias8"), ndma=1)
            P.add("act", lambda e: e.activation(out=bias8[:, :], in_=bias8[:, :], func=AF.Identity, scale=8.0),
                  reads=[res("bias8")], writes=[res("bias8")])
            for g, dil in enumerate(DILS):
                u = hp * 3 + g
                if u + 2 < 24:
                    wdma1(u + 2)
                wk, wres, _ = slot1(u)
                nb = (S // dil) // 128
                for which in range(3):
                    for tile in range(4):
                        bk = acc4()
                        proj_fm(wk, which * 128, tile, bk, wres)
                        if which == 0:
                            o, i_ = perm_views(qT1, tile, dil, ps[bk][:, :])
                            P.add("dve", lambda e, o=o, i_=i_: e.tensor_copy(out=o, in_=i_),
                                  reads=[pr[bk]], writes=[res("qT1")])
                            if g == 0 and hp > 0:
                                emit_combine(hp - 1, tile)
                        elif which == 1:
                            o, i_ = perm_views(kTA, tile, dil, ps[bk][:, :])
                            P.add("act", lambda e, o=o, i_=i_: e.activation(out=o[0:64], in_=i_[0:64], func=AF.Identity),
                                  reads=[pr[bk]], writes=[res("kT1")])
                            o2, i2 = perm_views(kTB, tile, dil, ps[bk][:, :])
                            P.add("dve", lambda e, o2=o2, i2=i2: e.tensor_copy(out=o2[64:128], in_=i2[64:128]),
                                  reads=[pr[bk]], writes=[res("kT1")])
                        else:
                            o, i_ = perm_views(vT1, tile, dil, ps[bk][:, :])
                            P.add("act", lambda e, o=o, i_=i_: e.activation(out=o, in_=i_, func=AF.Identity),
                                  reads=[pr[bk]], writes=[res("vT1")])
                for hb in range(2):
                    tb = 7 if hb == 0 else 2
                    pv = ps[tb][:, :].bitcast(BF16)
                    for j in range(8):
                        B = hb * 8 + j
                        P.add("pe", lambda e, B=B, j=j, pv=pv: e.transpose(
                            pv[:, j * 128:(j + 1) * 128], vT1[:, B * 128:(B + 1) * 128], ident),
                            reads=[res("vT1"), res("cstb")], writes=[pr[tb]])
                    P.add("dve", lambda e, hb=hb, pv=pv: e.tensor_copy(
                        out=vtm1[:, hb * 8:(hb + 1) * 8, :].rearrange("p s d -> p (s d)"), in_=pv[:, 0:1024]),
                        reads=[pr[tb]], writes=[res("vtm1")])

                if g == 2:
                    for tile in range(4):
                        tsl = slice(tile * 512, (tile + 1) * 512)
                        bk = acc_bank()
                        proj_fm(wgate, 0, tile, bk, rgate)
                        P.add("act", lambda e, bk=bk, tsl=tsl: e.activation(out=sg[:, tsl], in_=ps[bk][:, :], func=AF.Silu),
                              reads=[pr[bk]], writes=[res("sg")])
                    if hp + 1 < 8:
                        gdma(hp + 1)
                    if hp == 7:
                        load_wout(dwo_d, 1, list(range(8)))
                def emit_S(B):
                    n = B % nb
                    m0 = 0 if n > 0 else 1
                    sbk = 3 + (B % 2)
                    pts = B % 2
                    Sv = ps[sbk][:, :].rearrange("p (h m q) -> p h m q", h=2, m=2)
                    for hh in range(2):
                        kT_h = kTA if hh == 0 else kTB
                        P.add("pe", lambda e, Sv=Sv, hh=hh, m0=m0, g=g: e.matmul(
                            Sv[:, hh, m0:2, :], lhsT=ident, rhs=bv[:, g, hh, m0:2, :], start=True, stop=False),
                            reads=[res("bias8"), res("cstb")], writes=[pr[sbk]])
                        for mi in range(m0, 2):
                            m = B - 1 + mi
                            P.add("pe", lambda e, Sv=Sv, hh=hh, mi=mi, m=m, B=B, kT_h=kT_h: e.matmul(
                                Sv[:, hh, mi, :], lhsT=kT_h[:, m * 128:(m + 1) * 128], rhs=qT1[:, B * 128:(B + 1) * 128],
                                start=False, stop=(mi == 1)),
                                reads=[res("kT1"), res("qT1")], writes=[pr[sbk]])
                    PTv = PT[:, pts, :].rearrange("p (h m q) -> p h m q", h=2, m=2)
                    P.add("act", lambda e, Sv=Sv, PTv=PTv, m0=m0: e.activation(
                        out=PTv[:, :, m0:2, :], in_=Sv[:, :, m0:2, :], func=AF.Exp, scale=0.125),
                        reads=[pr[sbk]], writes=[res("PT%d" % pts)])

                def emit_PV(B, nbk, dbk):
                    n = B % nb
                    m0 = 0 if n > 0 else 1
                    pts = B % 2
                    PTv = PT[:, pts, :].rearrange("p (h m q) -> p h m q", h=2, m=2)
                    slot = B % 4
                    for hh in range(2):
                        for (obk, use_v) in ((nbk, True), (dbk, False)):
                            for mi in range(m0, 2):
                                m = B - 1 + mi
                                lhs = vtm1[:, m, hh * 64:(hh + 1) * 64] if use_v else ones64
                                P.add("pe", lambda e, obk=obk, hh=hh, mi=mi, lhs=lhs, PTv=PTv, slot=slot, m0=m0: e.matmul(
                                    ps[obk][hh * 64:(hh + 1) * 64, slot * 128:(slot + 1) * 128], lhsT=lhs, rhs=PTv[:, hh, mi, :],
                                    start=(mi == m0), stop=(mi == 1)),
                                    reads=[res("vtm1"), res("PT%d" % pts), res("cstb")], writes=[pr[obk]])

                emit_S(0)
                for B in range(16):
                    if B % 4 == 0:
                        nd_rot[0] ^= 1
                        nbk, dbk = (5, 7) if nd_rot[0] else (6, 2)
                    if B + 1 < 16:
                        emit_S(B + 1)
                    emit_PV(B, nbk, dbk)
                    if B % 4 == 3:
                        q4 = B // 4
                        for (obk, acc, aname) in ((nbk, accN, "accN"), (dbk, accD, "accD")):
                            if dil == 1:
                                dst = acc[:, q4 * 512:(q4 + 1) * 512]
                                src = ps[obk][:, :]
                            elif dil == 4:
                                dst = acc[:, :].rearrange("p (i r) -> p r i", r=4)[:, q4, :]
                                src = ps[obk][:, :]
                            else:
                                dst = acc[:, :].rearrange("p (i r) -> p r i", r=16)[:, q4 * 4:(q4 + 1) * 4, :]
                                src = ps[obk][:, :].rearrange("p (r i) -> p r i", r=4)
                            if g == 0:
                                P.add("dve", lambda e, dst=dst, src=src: e.tensor_copy(out=dst, in_=src),
                                      reads=[pr[obk]], writes=[res(aname)])
                            else:
                                P.add("dve", lambda e, dst=dst, src=src: e.tensor_tensor(out=dst, in0=src, in1=dst, op=ALU.add),
                                      reads=[pr[obk], res(aname)], writes=[res(aname)])
        for t_ in range(4):
            emit_combine(7, t_)
        outproj(1, b, wo_view, nxt=(("final",) if final else None),
                alias={"fs_sq": ["accD"], "hp_t": ["qT1", "kT1", "vT1"], "hp_rstd": ["vtm1"]})

    def final_norm(b):
        if final:
            P.barrier()
        for tile in range(4):
            tsl = slice(tile * 512, (tile + 1) * 512)
            half = tile % 2
            stg = ogT_f[:, half * 4096:(half + 1) * 4096].rearrange("p (k n) -> p k n", k=8)
            sres = ogr[half * 4:(half + 1) * 4]
            if final:
                rs, rres = stats(tile)
                for hf in range(2):
                    P.add("dve", lambda e, hf=hf, tsl=tsl, rs=rs: e.tensor_tensor(
                        out=hp_t[hf], in0=xT[:, 4 * hf:4 * hf + 4, tsl], in1=bc(rs, 1, 4), op=ALU.mult),
                        reads=xr2[tile][4 * hf:4 * hf + 4] + [rres], writes=[res("hp_t%d" % hf)])
                    for ci in range(4):
                        c = 4 * hf + ci
                        if ci % 2 == 0:
                            P.add("act", lambda e, c=c, ci=ci, hf=hf, stg=stg: e.activation(
                                out=stg[:, c, :], in_=hp_t[hf][:, ci, :], func=AF.Identity, scale=fgT[:, c:c + 1]),
                                reads=[res("hp_t%d" % hf), res("smallin")], writes=sres)
                        else:
                            P.add("dve", lambda e, c=c, ci=ci, hf=hf, stg=stg: e.tensor_scalar(
                                out=stg[:, c, :], in0=hp_t[hf][:, ci, :], scalar1=fgT[:, c:c + 1], scalar2=None, op0=ALU.mult),
                                reads=[res("hp_t%d" % hf), res("smallin")], writes=sres)
                st_ops.append(P.add("sync", lambda e, tsl=tsl, stg=stg: [e.dma_start(out=outT_d[b, :, :, tsl], in_=stg)],
                                    reads=sres, dma_res=res("outst%d" % half), ndma=1))
            else:
                st_ops.append(P.add("sync", lambda e, tsl=tsl: [e.dma_start(out=outT_d[b, :, :, tsl], in_=xT[:, :, tsl])],
                                    reads=xr, dma_res=res("outst%d" % half), ndma=1))

    for b in range(NB):
        for t_ in range(4):
            P.add("pool", lambda e, b=b, t_=t_: [e.dma_start(out=xT[:, :, t_ * 512:(t_ + 1) * 512], in_=xT_d[b, :, :, t_ * 512:(t_ + 1) * 512])],
                  writes=xr2[t_], dma_res=res("xload%d" % t_), ndma=1)
        if 0 in layers:
            gla_layer(b)
        if 1 in layers:
            dsw_layer(b)
        if not (1 in layers and final):
            final_norm(b)
    i = P.add("sync", None, reads=[], writes=[])
    P.ops[i].deps = list(st_ops)

    P.emit(nc, st)
    st.close()
    return nc, P


def _consts():
    c = np.zeros((128, 512), np.float32)
    c[:, 0:128] = np.eye(128, dtype=np.float32)
    j = np.arange(128)[:, None]
    t = np.arange(128)[None, :]
    same = (j // 64) == (t // 64)
    tri = same & (j <= t)
    c[:, 128:256] = np.where(tri, -1.0 / 16.0, 0.0)
    c[:, 256:384] = np.where(tri, 1.0, 0.0)
    c[:, 384:512] = 1.0
    return c


_CACHE = {}


def _get(key):
    if key not in _CACHE:
        layers, final = key
        _CACHE[key] = build(layers=layers, final=final)[0]
    return _CACHE[key]


def _fm(a):
    return a


def kernel(x, c, ada_w, ada_b, norm_g, gla_w_in, gla_w_alpha, gla_b_alpha, gla_norm_g,
           gla_w_out, dsw_w_in, dsw_w_out, rel_bias, final_g, _mode="fused"):
    f = np.float32
    x = np.asarray(x, f)
    ncore = 8
    common = {
        "ada_w": np.ascontiguousarray(np.asarray(ada_w, f).reshape(2, 8, 128, 3072).transpose(0, 2, 1, 3)),
        "ada_bT": np.ascontiguousarray(np.asarray(ada_b, f).reshape(2, 24, 128).transpose(2, 0, 1)),
        "norm_gT": np.ascontiguousarray(np.asarray(norm_g, f).reshape(2, 8, 128).transpose(2, 0, 1)),
        "final_gT": np.ascontiguousarray(np.asarray(final_g, f).reshape(8, 128).T),
        "consts": _consts(),
    }
    gw = np.asarray(gla_w_in, f)[0]
    wh = np.empty((4, 1024, 768), f)
    for h in range(4):
        wh[h, :, 0:128] = gw[:, h * 128:(h + 1) * 128]
        wh[h, :, 128:256] = gw[:, 512 + h * 128:512 + (h + 1) * 128]
        wh[h, :, 256:512] = gw[:, 1024 + h * 256:1024 + (h + 1) * 256]
        wh[h, :, 512:768] = gw[:, 2064 + h * 256:2064 + (h + 1) * 256]
    l0 = {
        "gla_wh": np.ascontiguousarray(wh.reshape(4, 8, 128, 3, 256).transpose(0, 3, 2, 1, 4)),
        "gla_wlr": np.ascontiguousarray(gw[:, 2048:2064].reshape(8, 128, 16).transpose(1, 0, 2)),
        "gla_walpha": np.ascontiguousarray(np.concatenate([np.asarray(gla_w_alpha, f)[0], np.asarray(gla_b_alpha, f)], axis=0)),
        "gla_gnT": np.ascontiguousarray(np.asarray(gla_norm_g, f)[0].reshape(2, 128).T),
        "gla_wo": np.ascontiguousarray(np.asarray(gla_w_out, f)[0].reshape(8, 128, 1024).transpose(1, 0, 2)),
    }
    dw = np.asarray(dsw_w_in, f)[0]
    wd = np.empty((8, 1024, 1280), f)
    for hp in range(8):
        for g in range(3):
            for which in range(3):
                src = g * 3072 + which * 1024 + hp * 128
                wd[hp, :, g * 384 + which * 128:g * 384 + (which + 1) * 128] = dw[:, src:src + 128]
        wd[hp, :, 1152:1280] = dw[:, 9216 + hp * 128:9216 + (hp + 1) * 128]
    l1 = {
        "dsw_wh": np.ascontiguousarray(wd[:, :, 0:1152].reshape(8, 8, 128, 3, 384).transpose(0, 3, 2, 1, 4)),
        "dsw_wg": np.ascontiguousarray(wd[:, :, 1152:1280].reshape(8, 8, 128, 128).transpose(0, 2, 1, 3)),
        "dsw_wo": np.ascontiguousarray(np.asarray(dsw_w_out, f)[0].reshape(8, 128, 1024).transpose(1, 0, 2)),
        "dsw_bias": bias_tables(np.asarray(rel_bias, f)).reshape(8, 128, 1536),
    }
    cT = np.asarray(c, f)

    def xT_of(arr, i):
        a = arr[2 * i:2 * i + 2]
        return np.ascontiguousarray(a.reshape(NB, S, 8, 128).transpose(0, 3, 2, 1))

    def cT_of(i):
        return np.ascontiguousarray(cT[2 * i:2 * i + 2].reshape(NB, 8, 128).transpose(2, 1, 0))

    def run(nc, extra, xin):
        maps = []
        for i in range(ncore):
            m = dict(common)
            m.update(extra)
            m["xT"] = xT_of(xin, i)
            m["cT"] = cT_of(i)
            maps.append(m)
        r = run_bass_kernel_spmd(nc, maps, core_ids=list(range(ncore)))
        out = np.empty((16, S, D), f)
        for i in range(ncore):
            o = np.asarray(r.results[i]["outT"])
            out[2 * i:2 * i + 2] = o.transpose(0, 3, 2, 1).reshape(NB, S, D)
        return out

    if _mode == "fused":
        ex = dict(l0)
        ex.update(l1)
        return run(_get(((0, 1), True)), ex, x)
    x1 = run(_get(((0,), False)), l0, x)
    if _mode == "l0":
        return x1
    return run(_get(((1,), True)), l1, x1)
```

```python
import math
from contextlib import ExitStack

import numpy as np
import concourse.bass as bass
import concourse.mybir as mybir
from concourse.bass_utils import run_bass_kernel_spmd

F32 = mybir.dt.float32
BF16 = mybir.dt.bfloat16
AF = mybir.ActivationFunctionType
ALU = mybir.AluOpType

S = 2048
D = 1024
NB = 2
EPS = 1e-6
NEG = -30000.0


class Res:
    __slots__ = ("name", "w", "r", "sem", "cnt")

    def __init__(self, name):
        self.name = name
        self.w = None
        self.r = []
        self.sem = None
        self.cnt = 0


class Op:
    __slots__ = ("eng", "fn", "deps", "kind", "sig", "val", "res", "ndma")


class Prog:
    ENG = ("sync", "act", "pool", "dve", "pe")

    def __init__(self):
        self.ops = []
        self.dma_res = []

    def add(self, eng, fn, reads=(), writes=(), dma_res=None, ndma=0):
        i = len(self.ops)
        op = Op()
        op.eng, op.fn, op.kind = eng, fn, ("d" if dma_res is not None else "c")
        op.sig, op.val, op.res, op.ndma = False, 0, dma_res, ndma
        deps = {}
        for r in reads:
            if r.w is not None:
                deps[r.w] = "raw"
        for w in writes:
            if w.w is not None and w.w not in deps:
                deps[w.w] = "waw"
            for j in w.r:
                if j not in deps:
                    deps[j] = "war"
        for j in [j for j in deps if self.ops[j].fn is None]:
            typ = deps.pop(j)
            if self.ops[j].eng == eng:
                continue
            for j2 in self.ops[j].deps:
                if j2 not in deps:
                    deps[j2] = "raw"
        out = []
        latest = {}
        for j, typ in deps.items():
            oj = self.ops[j]
            if oj.kind == "c" and oj.eng == eng and op.kind == "c":
                if eng == "pe":
                    continue
                if typ == "waw" and eng != "pool":
                    continue
            if oj.kind == "c":
                if latest.get(oj.eng, -1) < j:
                    latest[oj.eng] = j
            else:
                out.append(j)
        out.extend(latest.values())
        op.deps = out
        for r in reads:
            r.r.append(i)
        for w in writes:
            w.w = i
            w.r = []
        if dma_res is not None and dma_res not in self.dma_res:
            self.dma_res.append(dma_res)
        self.ops.append(op)
        return i

    def barrier(self):
        last = {}
        dmas = []
        for idx, op in enumerate(self.ops):
            if op.fn is None:
                continue
            if op.kind == "c":
                last[op.eng] = idx
            elif idx >= getattr(self, "bar_start", 0):
                dmas.append(idx)
        for eng in self.ENG:
            i = self.add(eng, None)
            self.ops[i].deps = [j for e2, j in last.items() if (e2 != eng or eng == "pool")] + list(dmas)
        self.bar_start = len(self.ops)

    def emit(self, nc, stack):
        ops = self.ops
        for op in ops:
            for j in op.deps:
                ops[j].sig = True
        esem = {e: stack.enter_context(nc.semaphore("sem_" + e)) for e in self.ENG}
        for r in self.dma_res:
            r.sem = stack.enter_context(nc.semaphore("dsem_" + r.name))
            r.cnt = 0
        ecnt = {e: 0 for e in self.ENG}
        for op in ops:
            if op.kind == "d":
                op.res.cnt += 16 * op.ndma
                op.val = op.res.cnt
            elif op.sig:
                ecnt[op.eng] += 1
                op.val = ecnt[op.eng]
        self.counts = dict(ecnt)
        by_eng = {e: [op for op in ops if op.eng == e] for e in self.ENG}

        def run(ename, e):
            waited = {}
            for op in by_eng[ename]:
                for j in op.deps:
                    oj = ops[j]
                    sem = oj.res.sem if oj.kind == "d" else esem[oj.eng]
                    key = sem.name
                    if waited.get(key, 0) >= oj.val:
                        continue
                    waited[key] = oj.val
                    e.wait_ge(sem, oj.val)
                if op.fn is None:
                    continue
                r = op.fn(e)
                if op.kind == "d":
                    assert len(r) == op.ndma
                    for ins in r:
                        ins.then_inc(op.res.sem, 16)
                elif op.sig:
                    r.then_inc(esem[op.eng], 1)

        block = stack.enter_context(nc.Block())

        @block.sync
        def _(e):
            run("sync", e)

        @block.scalar
        def _(e):
            run("act", e)

        @block.gpsimd
        def _(e):
            run("pool", e)

        @block.vector
        def _(e):
            run("dve", e)

        @block.tensor
        def _(e):
            run("pe", e)


def bc(ap, axis, n):
    a = [list(x) for x in ap.ap]
    a.insert(axis, [0, n])
    return bass.AP(ap.tensor, ap.offset, a)


def t5_bucket(n):
    max_exact = 16
    nf = np.maximum(n, 1).astype(np.float32)
    large = max_exact + (np.log(nf / max_exact) / math.log(2048 / max_exact) * (32 - max_exact)).astype(np.int32)
    large = np.minimum(large, 31)
    return np.where(n < max_exact, n, large).astype(np.int32)


DILS = (1, 4, 16)


def bias_tables(rel_bias):
    k = np.arange(128)[:, None]
    q = np.arange(128)[None, :]
    out = np.full((3, 16, 128, 2, 128), NEG, np.float32)
    for g, dil in enumerate(DILS):
        st = q + 128 - k
        valid = st <= 128
        b = t5_bucket(np.clip(st, 0, 128) * dil)
        for h in range(16):
            out[g, h, :, 0, :] = np.where(valid, rel_bias[b, h], NEG)
        st = q - k
        valid = st >= 0
        b = t5_bucket(np.clip(st, 0, 128) * dil)
        for h in range(16):
            out[g, h, :, 1, :] = np.where(valid, rel_bias[b, h], NEG)
    out = out.reshape(3, 8, 2, 128, 2, 128).transpose(1, 3, 0, 2, 4, 5)
    return np.ascontiguousarray(out)


def build(layers=(0, 1), final=True, dbg=False):
    nc = bass.Bass("TRN2", target_bir_lowering=False)
    st = ExitStack()
    P = Prog()

    def dram(name, shape, kind="ExternalInput", dt=F32):
        return nc.dram_tensor(name, list(shape), dt, kind=kind).ap()

    def sb(name, shape, dt):
        return st.enter_context(nc.sbuf_tensor(name, list(shape), dt))

    xT_d = dram("xT", [NB, 128, 8, S])
    cT_d = dram("cT", [128, 8, NB])
    adaw_d = dram("ada_w", [2, 128, 8, 3072])
    adab_d = dram("ada_bT", [128, 2, 24])
    ng_d = dram("norm_gT", [128, 2, 8])
    fg_d = dram("final_gT", [128, 8])
    cst_d = dram("consts", [128, 512])
    if 0 in layers:
        gw_d = dram("gla_wh", [4, 3, 128, 8, 256])
        gwl_d = dram("gla_wlr", [128, 8, 16])
        gwa_d = dram("gla_walpha", [17, 512])
        ggn_d = dram("gla_gnT", [128, 2])
        gwo_d = dram("gla_wo", [128, 8, 1024])
    if 1 in layers:
        dw_d = dram("dsw_wh", [8, 3, 128, 8, 384])
        dwg_d = dram("dsw_wg", [8, 128, 8, 128])
        dwo_d = dram("dsw_wo", [128, 8, 1024])
        db_d = dram("dsw_bias", [8, 128, 1536])
    outT_d = dram("outT", [NB, 128, 8, S], kind="ExternalOutput")

    xT = sb("xT_sb", [128, 8, S], F32)
    hT = sb("hT_sb", [128, 8, S], BF16)
    ogT = sb("ogT_sb", [128, 8, S], BF16)
    wbuf = sb("wbuf", [128, 8 * 1280], BF16)
    cst = sb("cst", [128, 512], F32)
    cstb = sb("cstb", [128, 512], BF16)
    c_sb = sb("c_sb", [128, 8, NB], F32)
    cact = sb("cact", [128, 8, NB], F32)
    adab = sb("adab", [128, 2, 24], F32)
    ngT = sb("ngT", [128, 2, 8], F32)
    fgT = sb("fgT", [128, 8], F32)
    modT = sb("modT", [128, 2, 24, NB], F32)
    gsT = sb("gsT", [128, 2, 8, NB], F32)
    ones_f = sb("ones_f", [128, 128], BF16)
    ones_hb = sb("ones_hb", [128, 128], BF16)

    ARENA16 = 24832 + 3072 - 1792 - 64 + 2048
    arena = sb("arena", [128, ARENA16], BF16)
    aoff = [0]

    def carve(shape, dt):
        n = 1
        for d_ in shape[1:]:
            n *= d_
        n16 = n * (2 if dt == F32 else 1)
        n16 = (n16 + 15) // 16 * 16
        assert aoff[0] + n16 <= ARENA16, (aoff[0], n16)
        v = arena[:, aoff[0]:aoff[0] + n16]
        aoff[0] += n16
        if dt == F32:
            v = v.bitcast(F32)
        v = v[:, 0:n]
        if len(shape) == 3:
            v = v.rearrange("p (a b) -> p a b", a=shape[1])
        return v

    R = {}

    def res(name):
        if name not in R:
            R[name] = Res(name)
        return R[name]

    wg = [res("wg%d" % i) for i in range(20)]
    xr2 = [[res("x%d_%d" % (t_, c)) for c in range(8)] for t_ in range(4)]
    xr = [r_ for t_ in xr2 for r_ in t_]
    hr = [res("h%d" % c) for c in range(8)]
    ogr = [res("og%d" % c) for c in range(8)]

    ps = [st.enter_context(nc.psum_tensor("ps%d" % i, [128, 512], F32)) for i in range(8)]
    pr = [res("psum%d" % i) for i in range(8)]
    acc_rot = [0]

    def acc_bank():
        acc_rot[0] ^= 1
        return acc_rot[0]

    ident = cstb[:, 0:128]
    ones64 = cstb[:, 384:448]
    U2 = cst[:, 128:256]
    maskT = cst[:, 256:384]

    P.add("sync", lambda e: [e.dma_start(out=cst[:], in_=cst_d),
                             e.dma_start(out=c_sb[:], in_=cT_d),
                             e.dma_start(out=adab[:], in_=adab_d),
                             e.dma_start(out=ngT[:], in_=ng_d),
                             e.dma_start(out=fgT[:], in_=fg_d)],
          writes=[res("cst"), res("smallin")], dma_res=res("cst"), ndma=5)
    P.add("dve", lambda e: e.tensor_copy(out=cstb[:], in_=cst[:]), reads=[res("cst")], writes=[res("cstb")])
    P.add("pool", lambda e: e.memset(ones_f[:], 1.0 / 1024.0), writes=[res("ones_f")])
    P.add("pool", lambda e: e.memset(ones_hb[:], 1.0 / 256.0), writes=[res("cstb")])
    P.add("act", lambda e: e.activation(out=cact[:], in_=c_sb[:], func=AF.Silu),
          reads=[res("cst")], writes=[res("cact")])

    ogT_f = ogT[:].rearrange("p c t -> p (c t)").bitcast(F32)
    mrow = arena[0:2, 20480:22528].bitcast(F32).rearrange("p (s n) -> p s n", s=2)
    def mod_piece(l, piece):
        pm = ps[3]
        half = piece % 2
        stg = ogT_f[:, half * 4096:(half + 1) * 4096].rearrange("p (k n) -> p k n", k=8)
        sres = ogr[half * 4:(half + 1) * 4]
        P.add("sync", lambda e, stg=stg, l=l, piece=piece: [
            e.dma_start(out=stg, in_=adaw_d[l, :, :, piece * 512:(piece + 1) * 512])],
            writes=sres, dma_res=res("adaw%d" % half), ndma=1)
        for k in range(8):
            P.add("pe", lambda e, stg=stg, k=k: e.matmul(
                ps[2][0:NB, :], lhsT=cact[:, k, :], rhs=stg[:, k, :], start=(k == 0), stop=(k == 7)),
                reads=sres + [res("cact")], writes=[pr[2]])
        P.add("act", lambda e, half=half: e.activation(out=mrow[:, half, :], in_=ps[2][0:NB, :], func=AF.Identity),
              reads=[pr[2]], writes=[res("mrow%d" % half)])
        for fi in range(4):
            f = piece * 4 + fi
            P.add("pe", lambda e, f=f, fi=fi, half=half, pm=pm: e.matmul(
                pm[:, f * NB:(f + 1) * NB], lhsT=mrow[:, half, fi * 128:(fi + 1) * 128], rhs=cst[0:NB, 0:NB],
                start=True, stop=True),
                reads=[res("mrow%d" % half), res("cst")], writes=[pr[3]])

    def mod_finish(l):
        pm = ps[3]
        P.add("dve", lambda e, l=l, pm=pm: e.tensor_tensor(
            out=modT[:, l, :, :], in0=pm[:, 0:24 * NB].rearrange("p (f b) -> p f b", b=NB),
            in1=bc(adab[:, l, :], 2, NB), op=ALU.add),
            reads=[pr[3], res("smallin")], writes=[res("mod%d" % l)])
        P.add("dve", lambda e, l=l: e.tensor_scalar(
            out=gsT[:, l, :, :], in0=modT[:, l, 8:16, :], scalar1=1.0, scalar2=None, op0=ALU.add),
            reads=[res("mod%d" % l)], writes=[res("gs%d" % l)])
        P.add("dve", lambda e, l=l: e.tensor_tensor(
            out=gsT[:, l, :, :], in0=gsT[:, l, :, :], in1=bc(ngT[:, l, :], 2, NB), op=ALU.mult),
            reads=[res("gs%d" % l), res("smallin")], writes=[res("gs%d" % l)])

    for piece in range(6):
        mod_piece(layers[0], piece)
    mod_finish(layers[0])
    deferred_mod = [(layers[1], p) for p in range(6)] if len(layers) > 1 else []

    st_ops = []
    cur_alias = {}

    def sr(name):
        return [res(name)]

    hp_sq = [arena[:, i * 4096:i * 4096 + 2048].rearrange("p (c t) -> p c t", c=4) for i in range(2)]
    hp_t = [arena[:, 8192 + i * 4096:8192 + (i + 1) * 4096].bitcast(F32).rearrange("p (c t) -> p c t", c=4) for i in range(2)]
    hp_rstd = [arena[:, 16384 + i * 1024:16384 + (i + 1) * 1024].bitcast(F32) for i in range(2)]

    def stats(tile):
        tsl = slice(tile * 512, (tile + 1) * 512)
        rs = hp_rstd[tile % 2]
        rname = "hp_rstd%d" % (tile % 2)
        for hf in range(2):
            P.add("act", lambda e, hf=hf: e.activation(out=hp_sq[hf], in_=xT[:, 4 * hf:4 * hf + 4, tsl], func=AF.Square),
                  reads=xr2[tile][4 * hf:4 * hf + 4], writes=[res("hp_sq%d" % hf)])
            for ci in range(4):
                c = 4 * hf + ci
                P.add("pe", lambda e, hf=hf, ci=ci, c=c: e.matmul(ps[2][:, :], lhsT=ones_f[:], rhs=hp_sq[hf][:, ci, :],
                                                                start=(c == 0), stop=(c == 7)),
                      reads=[res("hp_sq%d" % hf), res("ones_f")], writes=[pr[2]])
        P.add("act", lambda e, rs=rs: e.activation(out=rs, in_=ps[2][:, :], func=AF.Ln, bias=EPS),
              reads=[pr[2]], writes=[res(rname)])
        P.add("act", lambda e, rs=rs: e.activation(out=rs, in_=rs, func=AF.Exp, scale=-0.5),
              reads=[res(rname)], writes=[res(rname)])
        return rs, res(rname)

    def apply_h(l, b, tile, rs, rres):
            tsl = slice(tile * 512, (tile + 1) * 512)
            for hf in range(2):
                P.add("dve", lambda e, hf=hf, tsl=tsl, rs=rs: e.tensor_tensor(
                    out=hp_t[hf], in0=xT[:, 4 * hf:4 * hf + 4, tsl], in1=bc(rs, 1, 4), op=ALU.mult),
                    reads=xr2[tile][4 * hf:4 * hf + 4] + [rres], writes=sr("hp_t%d" % hf))
                for ci in range(4):
                    c = 4 * hf + ci
                    if ci % 2 == 0:
                        P.add("act", lambda e, c=c, ci=ci, hf=hf, tsl=tsl: e.activation(
                            out=hT[:, c, tsl], in_=hp_t[hf][:, ci, :], func=AF.Identity,
                            bias=modT[:, l, c, b:b + 1], scale=gsT[:, l, c, b:b + 1]),
                            reads=sr("hp_t%d" % hf) + [res("mod%d" % l), res("gs%d" % l)], writes=[hr[c]])
                    else:
                        P.add("dve", lambda e, c=c, ci=ci, hf=hf, tsl=tsl: e.tensor_scalar(
                            out=hT[:, c, tsl], in0=hp_t[hf][:, ci, :], scalar1=gsT[:, l, c, b:b + 1],
                            scalar2=modT[:, l, c, b:b + 1], op0=ALU.mult, op1=ALU.add),
                            reads=sr("hp_t%d" % hf) + [res("mod%d" % l), res("gs%d" % l)], writes=[hr[c]])

    def hprep(l, b):
        if b == 0 or 1 not in layers:
            P.barrier()
        else:
            P.add("act", None, writes=[res("hp_sq0"), res("accN")])
            P.add("act", None, writes=[res("hp_sq1"), res("accD")] + [res("fs_sq%d" % i_) for i_ in range(5)])
            P.add("dve", None, writes=[res("hp_t0"), res("qT1"), res("kT1")])
            P.add("dve", None, writes=[res("hp_t1"), res("kT1"), res("vT1")])
            P.add("act", None, writes=[res("hp_rstd0"), res("hp_rstd1"), res("vtm1")])
        if deferred_mod:
            mod_piece(*deferred_mod.pop(0))
        nxt_stats = stats(0)
        for tile in range(4):
            rs, rres = nxt_stats
            if tile + 1 < 4:
                if deferred_mod:
                    mod_piece(*deferred_mod.pop(0))
                nxt_stats = stats(tile + 1)
            apply_h(l, b, tile, rs, rres)
        if deferred_mod:
            l2 = deferred_mod[0][0]
            while deferred_mod:
                mod_piece(*deferred_mod.pop(0))
            mod_finish(l2)

    fs_sq = [arena[:, 4096 + i * 512:4096 + (i + 1) * 512] for i in range(5)]
    hT_f = hT[:].rearrange("p c t -> p (c t)").bitcast(F32)

    def apply_final(b, tile, rs, rres, stg, sres):
        tsl = slice(tile * 512, (tile + 1) * 512)
        for hf in range(2):
            P.add("dve", lambda e, hf=hf, tsl=tsl, rs=rs: e.tensor_tensor(
                out=hp_t[hf], in0=xT[:, 4 * hf:4 * hf + 4, tsl], in1=bc(rs, 1, 4), op=ALU.mult),
                reads=xr2[tile][4 * hf:4 * hf + 4] + [rres], writes=sr("hp_t%d" % hf))
            for ci in range(4):
                c = 4 * hf + ci
                if ci % 2 == 0:
                    P.add("act", lambda e, c=c, ci=ci, hf=hf, stg=stg: e.activation(
                        out=stg[:, c, :], in_=hp_t[hf][:, ci, :], func=AF.Identity, scale=fgT[:, c:c + 1]),
                        reads=sr("hp_t%d" % hf) + [res("smallin")], writes=sres)
                else:
                    P.add("dve", lambda e, c=c, ci=ci, hf=hf, stg=stg: e.tensor_scalar(
                        out=stg[:, c, :], in0=hp_t[hf][:, ci, :], scalar1=fgT[:, c:c + 1], scalar2=None, op0=ALU.mult),
                        reads=sr("hp_t%d" % hf) + [res("smallin")], writes=sres)
        st_ops.append(P.add("sync", lambda e, tsl=tsl, stg=stg: [e.dma_start(out=outT_d[b, :, :, tsl], in_=stg)],
                            reads=sres, dma_res=res("outst%d" % (tile % 2)), ndma=1))

    wo_view = wbuf[:, 0:8192].rearrange("p (e n) -> p e n", e=8)

    def load_wout(w_d, l, chunks):
        gr = []
        for e_ in chunks:
            gr += wg[2 * e_:2 * e_ + 2]
        P.add("pool", lambda e, chunks=tuple(chunks): [e.dma_start(out=wo_view[:, e_, :], in_=w_d[:, e_, :]) for e_ in chunks],
              writes=gr, dma_res=res("wo%d_%d" % (l, len(chunks))), ndma=len(chunks))
        return wo_view

    op_rot = [0]

    def outproj(l, b, wo, nxt=None, alias=None):
        LAG = 4
        cur_alias.clear()
        if nxt is not None:
            if alias is None:
                P.barrier()
            else:
                for eng_, names in (("act", ["fs_sq0", "fs_sq1", "fs_sq2", "fs_sq3", "fs_sq4", "hp_rstd0", "hp_rstd1"]),
                                    ("dve", ["hp_t0", "hp_t1"])):
                    for nm in names:
                        al = [x for pre, lst in alias.items() if nm.startswith(pre) for x in lst]
                        P.add(eng_, None, writes=[res(nm)] + [res(x) for x in al])
        for tile in range(4):
            tsl = slice(tile * 512, (tile + 1) * 512)

            def stat_mm(n):
                P.add("pe", lambda e, n=n: e.matmul(ps[2][:, :], lhsT=ones_f[:], rhs=fs_sq[n % 5],
                                                    start=(n == 0), stop=(n == 7)),
                      reads=sr("fs_sq%d" % (n % 5)) + [res("ones_f")], writes=[pr[2]])

            for n in range(8):
                op_rot[0] = (op_rot[0] + 1) % 7
                bk = (0, 1, 3, 4, 5, 6, 7)[op_rot[0]]
                for e_ in range(8):
                    P.add("pe", lambda e, e_=e_, n=n, bk=bk, tsl=tsl: e.matmul(
                        ps[bk][:, :], lhsT=wo[:, e_, n * 128:(n + 1) * 128], rhs=ogT[:, e_, tsl],
                        start=(e_ == 0), stop=(e_ == 7)),
                        reads=wg[0:16] + [ogr[e_]], writes=[pr[bk]])
                P.add("dve", lambda e, n=n, bk=bk, tsl=tsl: e.scalar_tensor_tensor(
                    out=xT[:, n, tsl], in0=ps[bk][:, :], scalar=modT[:, l, 16 + n, b:b + 1], in1=xT[:, n, tsl],
                    op0=ALU.mult, op1=ALU.add),
                    reads=[pr[bk], xr2[tile][n], res("mod%d" % l)], writes=[xr2[tile][n]])
                if nxt is not None:
                    P.add("act", lambda e, n=n, tsl=tsl: e.activation(out=fs_sq[n % 5], in_=xT[:, n, tsl], func=AF.Square),
                          reads=[xr2[tile][n]], writes=sr("fs_sq%d" % (n % 5)))
                    if n >= LAG:
                        stat_mm(n - LAG)
            if nxt is not None:
                for n in range(8 - LAG, 8):
                    stat_mm(n)
                rs = hp_rstd[tile % 2]
                rname = "hp_rstd%d" % (tile % 2)
                P.add("act", lambda e, rs=rs: e.activation(out=rs, in_=ps[2][:, :], func=AF.Ln, bias=EPS),
                      reads=[pr[2]], writes=sr(rname))
                P.add("act", lambda e, rs=rs: e.activation(out=rs, in_=rs, func=AF.Exp, scale=-0.5),
                      reads=sr(rname), writes=sr(rname))
                if nxt[0] == "hprep":
                    apply_h(nxt[1], b, tile, rs, res(rname))
                else:
                    half = tile % 2
                    stg = hT_f[:, half * 4096:(half + 1) * 4096].rearrange("p (k n) -> p k n", k=8)
                    apply_final(b, tile, rs, res(rname), stg, hr[half * 4:(half + 1) * 4])
        cur_alias.clear()

    def proj_fm(wk, col0, tile, bk, wres):
        tsl = slice(tile * 512, (tile + 1) * 512)
        for k in range(8):
            P.add("pe", lambda e, k=k: e.matmul(ps[bk][:, :], lhsT=wk[:, k, col0:col0 + 128], rhs=hT[:, k, tsl],
                                                start=(k == 0), stop=(k == 7)),
                  reads=wres + [hr[k]], writes=[pr[bk]])

    if 0 in layers:
        aoff[0] = 0
        ez = carve([128, 512], F32)
        lap = carve([128, 4, 128], F32)
        E1 = carve([128, 512], F32)
        E2 = carve([128, 512], F32)
        wst0 = arena[:, 0:4096].bitcast(F32).rearrange("p (s n) -> p s n", s=2)
        E3 = ez
        glrT = carve([128, 512], F32)
        qTs = carve([128, S], BF16)
        kTs = carve([128, S], BF16)
        kkvT = carve([128, S], BF16)
        vtm = carve([128, 16, 256], BF16)
        ktmA = carve([128, 16, 128], BF16)
        ktmB = carve([128, 16, 128], BF16)
        ract = carve([128, 4, 256], BF16)
        dec = carve([128, 32], F32)
        ATs = carve([128, 4, 128], BF16)
        Sall = arena[:, 1024:3584].bitcast(F32).rearrange("p (s e) -> p s e", s=5)
        Sbf = carve([128, 8, 256], BF16)
        osq = ez[:, 0:256].bitcast(BF16)
        rso = carve([128, 2, 128], F32)
        otmp = carve([128, 1024], F32)
        walp = carve([128, 512], F32)
        wlr = carve([128, 8, 16], BF16)
        gnT = carve([128, 2], F32)

    def gla_init():
        P.barrier()
        P.add("pool", lambda e: [e.dma_start(out=wlr, in_=gwl_d)], writes=[res("wlr")], dma_res=res("wlr"), ndma=1)
        P.add("sync", lambda e: [e.dma_start(out=walp[0:17, :], in_=gwa_d), e.dma_start(out=gnT, in_=ggn_d)],
              writes=[res("walp"), res("gnT")], dma_res=res("walp"), ndma=2)
        P.add("pool", lambda e: e.memset(glrT[0:32, :], 1.0), writes=[res("glrT")])
        P.add("pool", lambda e: e.memset(ktmA[:], 0.0), writes=[res("ktm")])
        P.add("pool", lambda e: e.memset(ktmB[:], 0.0), writes=[res("ktm")])

    def gla_layer(b):
        l = 0
        hprep(l, b)
        gla_init()
        def slot0(h, part):
            i = (3 * h + part) % 5
            return wbuf[:, i * 2048:(i + 1) * 2048].rearrange("p (k n) -> p k n", k=8), wg[4 * i:4 * i + 4], i

        def wdma0(h, part):
            ap, rs, i = slot0(h, part)
            P.add("pool", lambda e, ap=ap, h=h, part=part: [e.dma_start(out=ap, in_=gw_d[h, part])],
                  writes=rs, dma_res=res("w0slot%d" % i), ndma=1)

        for part in range(3):
            wdma0(0, part)
        for h in range(4):
            if h + 1 < 4:
                wdma0(h + 1, 0)
                wdma0(h + 1, 1)
            wA, rA, _ = slot0(h, 0)
            wV, rV, _ = slot0(h, 1)
            wR, rR, _ = slot0(h, 2)
            def emit_glr(tile):
                tsl = slice(tile * 512, (tile + 1) * 512)
                bk = acc_bank()
                for k in range(8):
                    P.add("pe", lambda e, k=k, bk=bk, tsl=tsl: e.matmul(
                        ps[bk][0:16, :], lhsT=wlr[:, k, :], rhs=hT[:, k, tsl], start=(k == 0), stop=(k == 7)),
                        reads=[res("wlr"), hr[k]], writes=[pr[bk]])
                P.add("dve", lambda e, bk=bk: e.tensor_copy(out=glrT[0:16, :], in_=ps[bk][0:16, :]),
                      reads=[pr[bk]], writes=[res("glrT")])

            def emit_z(tile, h=h):
                for s4 in range(4):
                    P.add("pe", lambda e, s4=s4, h=h: e.matmul(
                        ps[3][:, s4 * 128:(s4 + 1) * 128], lhsT=glrT[0:17, s4 * 128:(s4 + 1) * 128],
                        rhs=walp[0:17, h * 128:(h + 1) * 128], start=True, stop=True),
                        reads=[res("glrT"), res("walp")], writes=[pr[3]])
                P.add("act", lambda e: e.activation(out=ez[:], in_=ps[3][:, :], func=AF.Exp, scale=-1.0),
                      reads=[pr[3]], writes=[res("ez")])
                P.add("act", lambda e: e.activation(out=lap[:].rearrange("p s d -> p (s d)"), in_=ez[:], func=AF.Ln, bias=1.0),
                      reads=[res("ez")], writes=[res("lap")])

            def emit_bT(tile):
                for s4 in range(4):
                    P.add("pe", lambda e, s4=s4: e.matmul(
                        ps[4][:, s4 * 128:(s4 + 1) * 128], lhsT=lap[:, s4, :], rhs=U2, start=True, stop=True),
                        reads=[res("lap"), res("cst")], writes=[pr[4]])
                P.add("act", lambda e: e.activation(out=E1[:], in_=ps[4][:, :], func=AF.Exp, bias=math.log(128 ** -0.5)),
                      reads=[pr[4]], writes=[res("E1")])
                P.add("act", lambda e: e.activation(out=E2[:], in_=ps[4][:, :], func=AF.Exp, scale=-1.0),
                      reads=[pr[4]], writes=[res("E2")])
                P.add("act", lambda e, tile=tile: e.activation(
                    out=dec[:, tile * 8:(tile + 1) * 8],
                    in_=ps[4][:, :].rearrange("p (c j) -> p c j", j=64)[:, :, 63], func=AF.Exp),
                    reads=[pr[4]], writes=[res("dec")])
                P.add("dve", lambda e, tile=tile: e.tensor_tensor(
                    out=E3[:, :].rearrange("p (c j) -> p c j", j=64), in0=E2[:].rearrange("p (c j) -> p c j", j=64),
                    in1=bc(dec[:, tile * 8:(tile + 1) * 8], 2, 64), op=ALU.mult),
                    reads=[res("E2"), res("dec")], writes=[res("ez")])

            def emit_qk(tile, wA=wA, rA=rA):
                tsl = slice(tile * 512, (tile + 1) * 512)
                bk = acc_bank()
                proj_fm(wA, 0, tile, bk, rA)
                P.add("dve", lambda e, bk=bk, tsl=tsl: e.tensor_tensor(out=qTs[:, tsl], in0=ps[bk][:, :], in1=E1[:], op=ALU.mult),
                      reads=[pr[bk], res("E1")], writes=[res("qTs")])
                bk = acc_bank()
                proj_fm(wA, 128, tile, bk, rA)
                P.add("dve", lambda e, bk=bk, tsl=tsl: e.tensor_tensor(out=kTs[:, tsl], in0=ps[bk][:, :], in1=E2[:], op=ALU.mult),
                      reads=[pr[bk], res("E2")], writes=[res("kTs")])
                P.add("dve", lambda e, bk=bk, tsl=tsl: e.tensor_tensor(out=kkvT[:, tsl], in0=ps[bk][:, :], in1=E3[:], op=ALU.mult),
                      reads=[pr[bk], res("ez")], writes=[res("kkvT")])

            def emit_vhalf(tile, sp, wV=wV, rV=rV):
                bk = acc_bank()
                for s2 in range(2):
                    s_ = tile * 4 + sp * 2 + s2
                    for k in range(8):
                        P.add("pe", lambda e, k=k, s_=s_, s2=s2, bk=bk, wV=wV: e.matmul(
                            ps[bk][:, s2 * 256:(s2 + 1) * 256], lhsT=hT[:, k, s_ * 128:(s_ + 1) * 128],
                            rhs=wV[:, k, 0:256], start=(k == 0), stop=(k == 7)),
                            reads=rV + [hr[k]], writes=[pr[bk]])
                s0 = tile * 4 + sp * 2
                P.add("act", lambda e, bk=bk, s0=s0: e.activation(
                    out=vtm[:, s0:s0 + 2, :].rearrange("p s d -> p (s d)"), in_=ps[bk][:, :], func=AF.Identity),
                    reads=[pr[bk]], writes=[res("vtm")])

            def emit_tr(tile):
                pv = ps[5][:, :].bitcast(BF16)
                for s4 in range(4):
                    s_ = tile * 4 + s4
                    P.add("pe", lambda e, s_=s_, s4=s4, pv=pv: e.transpose(
                        pv[:, s4 * 128:(s4 + 1) * 128], kkvT[:, s_ * 128:(s_ + 1) * 128], ident),
                        reads=[res("kkvT"), res("cstb")], writes=[pr[5]])
                s0 = tile * 4
                P.add("dve", lambda e, pv=pv, s0=s0: e.tensor_copy(
                    out=ktmA[0:64, s0:s0 + 4, :], in_=pv[0:64, 0:512].rearrange("p (s d) -> p s d", s=4)),
                    reads=[pr[5]], writes=[res("ktm")])
                P.add("dve", lambda e, pv=pv, s0=s0: e.tensor_copy(
                    out=ktmB[64:128, s0:s0 + 4, :], in_=pv[64:128, 0:512].rearrange("p (s d) -> p s d", s=4)),
                    reads=[pr[5]], writes=[res("ktm")])

            for tile in range(4):
                emit_glr(tile)
                if tile > 0:
                    emit_tr(tile - 1)
                emit_vhalf(tile, 0)
                emit_z(tile)
                emit_vhalf(tile, 1)
                emit_bT(tile)
                emit_qk(tile)
            emit_tr(3)
            if h + 1 < 4:
                wdma0(h + 1, 2)
            else:
                load_wout(gwo_d, 0, [0, 1, 4, 5, 6, 7])
            SN = [res("lap"), res("E1"), res("E2")]
            P.add("pool", lambda e: e.memset(Sall[:, 0, :], 0.0), writes=SN)

            def emit_A(j):
                for sub in range(2):
                    s_ = 2 * j + sub
                    ssl = slice(s_ * 128, (s_ + 1) * 128)
                    P.add("pe", lambda e, ssl=ssl, sub=sub: e.matmul(
                        ps[5][:, sub * 128:(sub + 1) * 128], lhsT=kTs[:, ssl], rhs=qTs[:, ssl], start=True, stop=True),
                        reads=[res("kTs"), res("qTs")], writes=[pr[5]])

            def emit_mask(j):
                ab = j % 2
                P.add("dve", lambda e, ab=ab: e.tensor_tensor(
                    out=ATs[:, ab * 2:ab * 2 + 2, :], in0=ps[5][:, 0:256].rearrange("p (s t) -> p s t", s=2),
                    in1=bc(maskT, 1, 2), op=ALU.mult),
                    reads=[pr[5], res("cst")], writes=[res("ATs%d" % ab)])

            def emit_kv(j):
                for i in range(4):
                    c = 4 * j + i
                    s_ = c // 2
                    kb = 7 if i < 2 else 3
                    ktm_c = ktmA if c % 2 == 0 else ktmB
                    P.add("pe", lambda e, ktm_c=ktm_c, s_=s_, kb=kb, i=i: e.matmul(
                        ps[kb][:, (i % 2) * 256:(i % 2 + 1) * 256], lhsT=ktm_c[:, s_, :], rhs=vtm[:, s_, :], start=True, stop=True),
                        reads=[res("ktm"), res("vtm")], writes=[pr[kb]])

            def emit_scan(j):
                for i in range(4):
                    c = 4 * j + i
                    kb = 7 if i < 2 else 3
                    si, so = (i, i + 1) if j % 2 == 0 else (4 - i, 3 - i)
                    P.add("dve", lambda e, c=c, kb=kb, i=i, si=si, so=so: e.scalar_tensor_tensor(
                        out=Sall[:, so, :], in0=Sall[:, si, :], scalar=dec[:, c:c + 1],
                        in1=ps[kb][:, (i % 2) * 256:(i % 2 + 1) * 256], op0=ALU.mult, op1=ALU.add),
                        reads=SN + [pr[kb], res("dec")], writes=SN)

            def emit_copies(j):
                lo = 0 if j % 2 == 0 else 1
                P.add("dve", lambda e, j=j, lo=lo: e.tensor_copy(
                    out=Sbf[:, (j % 2) * 4:(j % 2) * 4 + 4, :], in_=Sall[:, lo:lo + 4, :]),
                    reads=SN, writes=[res("Sbf%d" % (j % 2))])

            def emit_rproj(j):
                t0 = j * 256
                bk = acc_bank()
                for half in range(2):
                    for k in range(8):
                        P.add("pe", lambda e, k=k, half=half, bk=bk, t0=t0, wR=wR: e.matmul(
                            ps[bk][:, half * 256:(half + 1) * 256], lhsT=wR[:, k, half * 128:(half + 1) * 128],
                            rhs=hT[:, k, t0:t0 + 256], start=(k == 0), stop=(k == 7)),
                            reads=rR + [hr[k]], writes=[pr[bk]])
                rb = j % 2
                sgt = otmp[:, rb * 512:(rb + 1) * 512]
                P.add("act", lambda e, bk=bk, sgt=sgt: e.activation(out=sgt, in_=ps[bk][:, :], func=AF.Exp, scale=-1.0),
                      reads=[pr[bk]], writes=[res("otmp%d" % rb)])
                P.add("act", lambda e, sgt=sgt: e.activation(out=sgt, in_=sgt, func=AF.Ln, bias=1.0),
                      reads=[res("otmp%d" % rb)], writes=[res("otmp%d" % rb)])
                P.add("act", lambda e, sgt=sgt: e.activation(out=sgt, in_=sgt, func=AF.Exp, scale=-1.0),
                      reads=[res("otmp%d" % rb)], writes=[res("otmp%d" % rb)])
                for half in range(2):
                    P.add("dve", lambda e, bk=bk, rb=rb, sgt=sgt, half=half: e.scalar_tensor_tensor(
                        out=ract[:, 2 * rb + half, :], in0=ps[bk][:, half * 256:(half + 1) * 256], scalar=gnT[:, half:half + 1],
                        in1=sgt[:, half * 256:(half + 1) * 256], op0=ALU.mult, op1=ALU.mult),
                        reads=[pr[bk], res("otmp%d" % rb), res("gnT")], writes=[res("ract%d" % rb)])

            def emit_oT(j, j_=None):
                ab = j % 2
                ob = 6 if j % 2 == 0 else 4
                first = True
                for sub in range(2):
                    s_ = 2 * j + sub
                    for half in range(2):
                        ocol = (sub * 2 + half) * 128
                        P.add("pe", lambda e, half=half, ocol=ocol, ab=ab, sub=sub, s_=s_, first=first, ob=ob: e.matmul(
                            ps[ob][:, ocol:ocol + 128], lhsT=vtm[:, s_, half * 128:(half + 1) * 128],
                            rhs=ATs[:, ab * 2 + sub, :], start=first, stop=False, skip_group_check=True),
                            reads=[res("vtm"), res("ATs%d" % ab)], writes=[pr[ob]])
                        first = False
                        for cc in range(2):
                            c = 2 * s_ + cc
                            i = c - 4 * j
                            if c == 0:
                                continue
                            P.add("pe", lambda e, half=half, ocol=ocol, cc=cc, c=c, i=i, ob=ob: e.matmul(
                                ps[ob][:, ocol + cc * 64:ocol + (cc + 1) * 64], lhsT=Sbf[:, (j % 2) * 4 + (i if j % 2 == 0 else 3 - i), half * 128:(half + 1) * 128],
                                rhs=qTs[:, c * 64:(c + 1) * 64], start=False, stop=(cc == 1), skip_group_check=True),
                                reads=[res("Sbf%d" % (j % 2)), res("qTs")], writes=[pr[ob]])

            def emit_post1(j):
                ob = 6 if j % 2 == 0 else 4
                P.add("act", lambda e, ob=ob: e.activation(out=osq[:], in_=ps[ob][:, :], func=AF.Square),
                      reads=[pr[ob]], writes=[res("ez")])

            def emit_post2(j):
                t0 = j * 256
                ob = 6 if j % 2 == 0 else 4
                for sb_ in range(2):
                    for half in range(2):
                        col = (sb_ * 2 + half) * 128
                        P.add("pe", lambda e, sb_=sb_, half=half, col=col: e.matmul(
                            ps[2][:, sb_ * 128:(sb_ + 1) * 128], lhsT=ones_hb[:], rhs=osq[:, col:col + 128],
                            start=(half == 0), stop=(half == 1)),
                            reads=[res("ez"), res("cstb")], writes=[pr[2]])
                P.add("act", lambda e: e.activation(out=rso[:].rearrange("p s t -> p (s t)"), in_=ps[2][:, 0:256], func=AF.Ln, bias=EPS),
                      reads=[pr[2]], writes=[res("rso")])
                P.add("act", lambda e: e.activation(out=rso[:].rearrange("p s t -> p (s t)"), in_=rso[:].rearrange("p s t -> p (s t)"),
                                                    func=AF.Exp, scale=-0.5),
                      reads=[res("rso")], writes=[res("rso")])
                rb = j % 2
                ot = otmp[:, rb * 512:(rb + 1) * 512]
                P.add("dve", lambda e, ob=ob, ot=ot: e.tensor_tensor(
                    out=ot.rearrange("p (s h t) -> p s h t", s=2, h=2),
                    in0=ps[ob][:, :].rearrange("p (s h t) -> p s h t", s=2, h=2),
                    in1=bc(rso[:], 2, 2), op=ALU.mult),
                    reads=[pr[ob], res("rso")], writes=[res("otmp%d" % rb)])
                for half in range(2):
                    P.add("pool", lambda e, t0=t0, h=h, rb=rb, ot=ot, half=half: e.tensor_tensor(
                        out=ogT[:, 2 * h + half, t0:t0 + 256].rearrange("p (s t) -> p s t", s=2),
                        in0=ot.rearrange("p (s h t) -> p s h t", s=2, h=2)[:, :, half, :],
                        in1=ract[:, 2 * rb + half, :].rearrange("p (s t) -> p s t", s=2), op=ALU.mult),
                        reads=[res("otmp%d" % rb), res("ract%d" % rb)], writes=[ogr[2 * h + half]])

            emit_A(0)
            emit_kv(0)
            emit_scan(0)
            emit_copies(0)
            emit_mask(0)
            for j in range(8):
                if j + 1 < 8:
                    emit_A(j + 1)
                    emit_mask(j + 1)
                    emit_kv(j + 1)
                    emit_scan(j + 1)
                    emit_copies(j + 1)
                emit_rproj(j)
                if j > 0:
                    emit_post2(j - 1)
                emit_oT(j)
                emit_post1(j)
            emit_post2(7)
        wo = load_wout(gwo_d, 0, [2, 3])
        outproj(0, b, wo, nxt=(("hprep", 1) if 1 in layers else None),
                alias={"fs_sq": ["glrT", "qTs"], "hp_t": ["kTs", "kkvT", "vtm", "ktm"], "hp_rstd": ["ktm"]})

    if 1 in layers:
        aoff[0] = 0
        accN = carve([128, S], F32)
        wst1 = arena[:, 0:4096].bitcast(F32).rearrange("p (s n) -> p s n", s=2)
        accD = carve([128, S], F32)
        qT1 = carve([128, S], BF16)
        kTA = carve([128, S], BF16)
        kTB = carve([128, S], BF16)
        vT1 = carve([128, S], BF16)
        vtm1 = carve([128, 16, 128], BF16)
        sg = carve([128, S], BF16)
        bias8 = carve([128, 1536], BF16)
        PT = carve([128, 2, 512], BF16)

    def dsw_init():
        P.barrier()
        P.add("pool", lambda e: e.memset(kTA[:], 0.0), writes=[res("kT1")])
        P.add("pool", lambda e: e.memset(kTB[:], 0.0), writes=[res("kT1")])

    def perm_views(dst, tile, dil, psrc):
        if dil == 1:
            return dst[:, tile * 512:(tile + 1) * 512], psrc
        L = S // dil
        n_i = 512 // dil
        o = dst[:, :].rearrange("p (r i) -> p r i", r=dil)[:, :, tile * n_i:(tile + 1) * n_i]
        i_ = psrc.rearrange("p (i r) -> p r i", r=dil)
        return o, i_

    def dsw_layer(b):
        l = 1
        if 0 not in layers:
            hprep(l, b)
        dsw_init()
        bv = bias8[:, :].rearrange("p (g h m q) -> p g h m q", g=3, h=2, m=2)
        wgate = wbuf[:, 9216:10240].rearrange("p (k n) -> p k n", k=8)
        rgate = wg[18:20]

        def slot1(u):
            j = u % 3
            return wbuf[:, j * 3072:(j + 1) * 3072].rearrange("p (k n) -> p k n", k=8), wg[6 * j:6 * j + 6], j

        def wdma1(u):
            ap, rs, j = slot1(u)
            P.add("pool", lambda e, ap=ap, u=u: [e.dma_start(out=ap, in_=dw_d[u // 3, u % 3])],
                  writes=rs, dma_res=res("w1slot%d" % j), ndma=1)

        def gdma(hp):
            P.add("pool", lambda e, hp=hp: [e.dma_start(out=wgate, in_=dwg_d[hp])],
                  writes=rgate, dma_res=res("w1gate"), ndma=1)

        gdma(0)
        wdma1(0)
        wdma1(1)
        def emit_combine(hp, t):
            ts_ = slice(t * 512, (t + 1) * 512)
            P.add("act", lambda e, ts_=ts_: e.activation(out=accD[:, ts_], in_=accD[:, ts_], func=AF.Ln),
                  reads=[res("accD")], writes=[res("accD")])
            P.add("act", lambda e, ts_=ts_: e.activation(out=accD[:, ts_], in_=accD[:, ts_], func=AF.Exp, scale=-1.0),
                  reads=[res("accD")], writes=[res("accD")])
            P.add("dve", lambda e, ts_=ts_: e.tensor_tensor(out=accN[:, ts_], in0=accN[:, ts_], in1=accD[:, ts_], op=ALU.mult),
                  reads=[res("accN"), res("accD")], writes=[res("accN")])
            P.add("dve", lambda e, hp=hp, ts_=ts_: e.tensor_tensor(out=ogT[:, hp, ts_], in0=accN[:, ts_], in1=sg[:, ts_], op=ALU.mult),
                  reads=[res("accN"), res("sg")], writes=[ogr[hp]])

        nd_rot = [0]
        acc4_rot = [0]

        def acc4():
            acc4_rot[0] = (acc4_rot[0] + 1) % 4
            return (0, 1, 3, 4)[acc4_rot[0]]

        for hp in range(8):
            P.add("pool", lambda e, hp=hp: [e.dma_start(out=bias8, in_=db_d[hp])],
                  writes=[res("bias8")], dma_res=res("bias8"), ndma=1)
            P.add("act", lambda e: e.activation(out=bias8[:, :], in_=bias8[:, :], func=AF.Identity, scale=8.0),
                  reads=[res("bias8")], writes=[res("bias8")])
            for g, dil in enumerate(DILS):
                u = hp * 3 + g
                if u + 2 < 24:
                    wdma1(u + 2)
                wk, wres, _ = slot1(u)
                nb = (S // dil) // 128
                for which in range(3):
                    for tile in range(4):
                        bk = acc4()
                        proj_fm(wk, which * 128, tile, bk, wres)
                        if which == 0:
                            o, i_ = perm_views(qT1, tile, dil, ps[bk][:, :])
                            P.add("dve", lambda e, o=o, i_=i_: e.tensor_copy(out=o, in_=i_),
                                  reads=[pr[bk]], writes=[res("qT1")])
                            if g == 0 and hp > 0:
                                emit_combine(hp - 1, tile)
                        elif which == 1:
                            o, i_ = perm_views(kTA, tile, dil, ps[bk][:, :])
                            P.add("act", lambda e, o=o, i_=i_: e.activation(out=o[0:64], in_=i_[0:64], func=AF.Identity),
                                  reads=[pr[bk]], writes=[res("kT1")])
                            o2, i2 = perm_views(kTB, tile, dil, ps[bk][:, :])
                            P.add("dve", lambda e, o2=o2, i2=i2: e.tensor_copy(out=o2[64:128], in_=i2[64:128]),
                                  reads=[pr[bk]], writes=[res("kT1")])
                        else:
                            o, i_ = perm_views(vT1, tile, dil, ps[bk][:, :])
                            P.add("act", lambda e, o=o, i_=i_: e.activation(out=o, in_=i_, func=AF.Identity),
                                  reads=[pr[bk]], writes=[res("vT1")])
                for hb in range(2):
                    tb = 7 if hb == 0 else 2
                    pv = ps[tb][:, :].bitcast(BF16)
                    for j in range(8):
                        B = hb * 8 + j
                        P.add("pe", lambda e, B=B, j=j, pv=pv: e.transpose(
                            pv[:, j * 128:(j + 1) * 128], vT1[:, B * 128:(B + 1) * 128], ident),
                            reads=[res("vT1"), res("cstb")], writes=[pr[tb]])
                    P.add("dve", lambda e, hb=hb, pv=pv: e.tensor_copy(
                        out=vtm1[:, hb * 8:(hb + 1) * 8, :].rearrange("p s d -> p (s d)"), in_=pv[:, 0:1024]),
                        reads=[pr[tb]], writes=[res("vtm1")])

                if g == 2:
                    for tile in range(4):
                        tsl = slice(tile * 512, (tile + 1) * 512)
                        bk = acc_bank()
                        proj_fm(wgate, 0, tile, bk, rgate)
                        P.add("act", lambda e, bk=bk, tsl=tsl: e.activation(out=sg[:, tsl], in_=ps[bk][:, :], func=AF.Silu),
                              reads=[pr[bk]], writes=[res("sg")])
                    if hp + 1 < 8:
                        gdma(hp + 1)
                    if hp == 7:
                        load_wout(dwo_d, 1, list(range(8)))
                def emit_S(B):
                    n = B % nb
                    m0 = 0 if n > 0 else 1
                    sbk = 3 + (B % 2)
                    pts = B % 2
                    Sv = ps[sbk][:, :].rearrange("p (h m q) -> p h m q", h=2, m=2)
                    for hh in range(2):
                        kT_h = kTA if hh == 0 else kTB
                        P.add("pe", lambda e, Sv=Sv, hh=hh, m0=m0, g=g: e.matmul(
                            Sv[:, hh, m0:2, :], lhsT=ident, rhs=bv[:, g, hh, m0:2, :], start=True, stop=False),
                            reads=[res("bias8"), res("cstb")], writes=[pr[sbk]])
                        for mi in range(m0, 2):
                            m = B - 1 + mi
                            P.add("pe", lambda e, Sv=Sv, hh=hh, mi=mi, m=m, B=B, kT_h=kT_h: e.matmul(
                                Sv[:, hh, mi, :], lhsT=kT_h[:, m * 128:(m + 1) * 128], rhs=qT1[:, B * 128:(B + 1) * 128],
                                start=False, stop=(mi == 1)),
                                reads=[res("kT1"), res("qT1")], writes=[pr[sbk]])
                    PTv = PT[:, pts, :].rearrange("p (h m q) -> p h m q", h=2, m=2)
                    P.add("act", lambda e, Sv=Sv, PTv=PTv, m0=m0: e.activation(
                        out=PTv[:, :, m0:2, :], in_=Sv[:, :, m0:2, :], func=AF.Exp, scale=0.125),
                        reads=[pr[sbk]], writes=[res("PT%d" % pts)])

                def emit_PV(B, nbk, dbk):
                    n = B % nb
                    m0 = 0 if n > 0 else 1
                    pts = B % 2
                    PTv = PT[:, pts, :].rearrange("p (h m q) -> p h m q", h=2, m=2)
                    slot = B % 4
                    for hh in range(2):
                        for (obk, use_v) in ((nbk, True), (dbk, False)):
                            for mi in range(m0, 2):
                                m = B - 1 + mi
                                lhs = vtm1[:, m, hh * 64:(hh + 1) * 64] if use_v else ones64
                                P.add("pe", lambda e, obk=obk, hh=hh, mi=mi, lhs=lhs, PTv=PTv, slot=slot, m0=m0: e.matmul(
                                    ps[obk][hh * 64:(hh + 1) * 64, slot * 128:(slot + 1) * 128], lhsT=lhs, rhs=PTv[:, hh, mi, :],
                                    start=(mi == m0), stop=(mi == 1)),
                                    reads=[res("vtm1"), res("PT%d" % pts), res("cstb")], writes=[pr[obk]])

                emit_S(0)
                for B in range(16):
                    if B % 4 == 0:
                        nd_rot[0] ^= 1
                        nbk, dbk = (5, 7) if nd_rot[0] else (6, 2)
                    if B + 1 < 16:
                        emit_S(B + 1)
                    emit_PV(B, nbk, dbk)
                    if B % 4 == 3:
                        q4 = B // 4
                        for (obk, acc, aname) in ((nbk, accN, "accN"), (dbk, accD, "accD")):
                            if dil == 1:
                                dst = acc[:, q4 * 512:(q4 + 1) * 512]
                                src = ps[obk][:, :]
                            elif dil == 4:
                                dst = acc[:, :].rearrange("p (i r) -> p r i", r=4)[:, q4, :]
                                src = ps[obk][:, :]
                            else:
                                dst = acc[:, :].rearrange("p (i r) -> p r i", r=16)[:, q4 * 4:(q4 + 1) * 4, :]
                                src = ps[obk][:, :].rearrange("p (r i) -> p r i", r=4)
                            if g == 0:
                                P.add("dve", lambda e, dst=dst, src=src: e.tensor_copy(out=dst, in_=src),
                                      reads=[pr[obk]], writes=[res(aname)])
                            else:
                                P.add("dve", lambda e, dst=dst, src=src: e.tensor_tensor(out=dst, in0=src, in1=dst, op=ALU.add),
                                      reads=[pr[obk], res(aname)], writes=[res(aname)])
        for t_ in range(4):
            emit_combine(7, t_)
        outproj(1, b, wo_view, nxt=(("final",) if final else None),
                alias={"fs_sq": ["accD"], "hp_t": ["qT1", "kT1", "vT1"], "hp_rstd": ["vtm1"]})

    def final_norm(b):
        if final:
            P.barrier()
        for tile in range(4):
            tsl = slice(tile * 512, (tile + 1) * 512)
            half = tile % 2
            stg = ogT_f[:, half * 4096:(half + 1) * 4096].rearrange("p (k n) -> p k n", k=8)
            sres = ogr[half * 4:(half + 1) * 4]
            if final:
                rs, rres = stats(tile)
                for hf in range(2):
                    P.add("dve", lambda e, hf=hf, tsl=tsl, rs=rs: e.tensor_tensor(
                        out=hp_t[hf], in0=xT[:, 4 * hf:4 * hf + 4, tsl], in1=bc(rs, 1, 4), op=ALU.mult),
                        reads=xr2[tile][4 * hf:4 * hf + 4] + [rres], writes=[res("hp_t%d" % hf)])
                    for ci in range(4):
                        c = 4 * hf + ci
                        if ci % 2 == 0:
                            P.add("act", lambda e, c=c, ci=ci, hf=hf, stg=stg: e.activation(
                                out=stg[:, c, :], in_=hp_t[hf][:, ci, :], func=AF.Identity, scale=fgT[:, c:c + 1]),
                                reads=[res("hp_t%d" % hf), res("smallin")], writes=sres)
                        else:
                            P.add("dve", lambda e, c=c, ci=ci, hf=hf, stg=stg: e.tensor_scalar(
                                out=stg[:, c, :], in0=hp_t[hf][:, ci, :], scalar1=fgT[:, c:c + 1], scalar2=None, op0=ALU.mult),
                                reads=[res("hp_t%d" % hf), res("smallin")], writes=sres)
                st_ops.append(P.add("sync", lambda e, tsl=tsl, stg=stg: [e.dma_start(out=outT_d[b, :, :, tsl], in_=stg)],
                                    reads=sres, dma_res=res("outst%d" % half), ndma=1))
            else:
                st_ops.append(P.add("sync", lambda e, tsl=tsl: [e.dma_start(out=outT_d[b, :, :, tsl], in_=xT[:, :, tsl])],
                                    reads=xr, dma_res=res("outst%d" % half), ndma=1))

    for b in range(NB):
        for t_ in range(4):
            P.add("pool", lambda e, b=b, t_=t_: [e.dma_start(out=xT[:, :, t_ * 512:(t_ + 1) * 512], in_=xT_d[b, :, :, t_ * 512:(t_ + 1) * 512])],
                  writes=xr2[t_], dma_res=res("xload%d" % t_), ndma=1)
        if 0 in layers:
            gla_layer(b)
        if 1 in layers:
            dsw_layer(b)
        if not (1 in layers and final):
            final_norm(b)
    i = P.add("sync", None, reads=[], writes=[])
    P.ops[i].deps = list(st_ops)

    P.emit(nc, st)
    st.close()
    return nc, P


def _consts():
    c = np.zeros((128, 512), np.float32)
    c[:, 0:128] = np.eye(128, dtype=np.float32)
    j = np.arange(128)[:, None]
    t = np.arange(128)[None, :]
    same = (j // 64) == (t // 64)
    tri = same & (j <= t)
    c[:, 128:256] = np.where(tri, -1.0 / 16.0, 0.0)
    c[:, 256:384] = np.where(tri, 1.0, 0.0)
    c[:, 384:512] = 1.0
    return c


_CACHE = {}


def _get(key):
    if key not in _CACHE:
        layers, final = key
        _CACHE[key] = build(layers=layers, final=final)[0]
    return _CACHE[key]


def _fm(a):
    return a


def kernel(x, c, ada_w, ada_b, norm_g, gla_w_in, gla_w_alpha, gla_b_alpha, gla_norm_g,
           gla_w_out, dsw_w_in, dsw_w_out, rel_bias, final_g, _mode="fused"):
    f = np.float32
    x = np.asarray(x, f)
    ncore = 8
    common = {
        "ada_w": np.ascontiguousarray(np.asarray(ada_w, f).reshape(2, 8, 128, 3072).transpose(0, 2, 1, 3)),
        "ada_bT": np.ascontiguousarray(np.asarray(ada_b, f).reshape(2, 24, 128).transpose(2, 0, 1)),
        "norm_gT": np.ascontiguousarray(np.asarray(norm_g, f).reshape(2, 8, 128).transpose(2, 0, 1)),
        "final_gT": np.ascontiguousarray(np.asarray(final_g, f).reshape(8, 128).T),
        "consts": _consts(),
    }
    gw = np.asarray(gla_w_in, f)[0]
    wh = np.empty((4, 1024, 768), f)
    for h in range(4):
        wh[h, :, 0:128] = gw[:, h * 128:(h + 1) * 128]
        wh[h, :, 128:256] = gw[:, 512 + h * 128:512 + (h + 1) * 128]
        wh[h, :, 256:512] = gw[:, 1024 + h * 256:1024 + (h + 1) * 256]
        wh[h, :, 512:768] = gw[:, 2064 + h * 256:2064 + (h + 1) * 256]
    l0 = {
        "gla_wh": np.ascontiguousarray(wh.reshape(4, 8, 128, 3, 256).transpose(0, 3, 2, 1, 4)),
        "gla_wlr": np.ascontiguousarray(gw[:, 2048:2064].reshape(8, 128, 16).transpose(1, 0, 2)),
        "gla_walpha": np.ascontiguousarray(np.concatenate([np.asarray(gla_w_alpha, f)[0], np.asarray(gla_b_alpha, f)], axis=0)),
        "gla_gnT": np.ascontiguousarray(np.asarray(gla_norm_g, f)[0].reshape(2, 128).T),
        "gla_wo": np.ascontiguousarray(np.asarray(gla_w_out, f)[0].reshape(8, 128, 1024).transpose(1, 0, 2)),
    }
    dw = np.asarray(dsw_w_in, f)[0]
    wd = np.empty((8, 1024, 1280), f)
    for hp in range(8):
        for g in range(3):
            for which in range(3):
                src = g * 3072 + which * 1024 + hp * 128
                wd[hp, :, g * 384 + which * 128:g * 384 + (which + 1) * 128] = dw[:, src:src + 128]
        wd[hp, :, 1152:1280] = dw[:, 9216 + hp * 128:9216 + (hp + 1) * 128]
    l1 = {
        "dsw_wh": np.ascontiguousarray(wd[:, :, 0:1152].reshape(8, 8, 128, 3, 384).transpose(0, 3, 2, 1, 4)),
        "dsw_wg": np.ascontiguousarray(wd[:, :, 1152:1280].reshape(8, 8, 128, 128).transpose(0, 2, 1, 3)),
        "dsw_wo": np.ascontiguousarray(np.asarray(dsw_w_out, f)[0].reshape(8, 128, 1024).transpose(1, 0, 2)),
        "dsw_bias": bias_tables(np.asarray(rel_bias, f)).reshape(8, 128, 1536),
    }
    cT = np.asarray(c, f)

    def xT_of(arr, i):
        a = arr[2 * i:2 * i + 2]
        return np.ascontiguousarray(a.reshape(NB, S, 8, 128).transpose(0, 3, 2, 1))

    def cT_of(i):
        return np.ascontiguousarray(cT[2 * i:2 * i + 2].reshape(NB, 8, 128).transpose(2, 1, 0))

    def run(nc, extra, xin):
        maps = []
        for i in range(ncore):
            m = dict(common)
            m.update(extra)
            m["xT"] = xT_of(xin, i)
            m["cT"] = cT_of(i)
            maps.append(m)
        r = run_bass_kernel_spmd(nc, maps, core_ids=list(range(ncore)))
        out = np.empty((16, S, D), f)
        for i in range(ncore):
            o = np.asarray(r.results[i]["outT"])
            out[2 * i:2 * i + 2] = o.transpose(0, 3, 2, 1).reshape(NB, S, D)
        return out

    if _mode == "fused":
        ex = dict(l0)
        ex.update(l1)
        return run(_get(((0, 1), True)), ex, x)
    x1 = run(_get(((0,), False)), l0, x)
    if _mode == "l0":
        return x1
    return run(_get(((1,), True)), l1, x1)
```

```python
import math
from contextlib import ExitStack

import numpy as np
import concourse.bass as bass
import concourse.mybir as mybir
from concourse.bass_utils import run_bass_kernel_spmd

F32 = mybir.dt.float32
BF16 = mybir.dt.bfloat16
AF = mybir.ActivationFunctionType
ALU = mybir.AluOpType

S = 2048
D = 1024
NB = 2
EPS = 1e-6
NEG = -30000.0


class Res:
    __slots__ = ("name", "w", "r", "sem", "cnt")

    def __init__(self, name):
        self.name = name
        self.w = None
        self.r = []
        self.sem = None
        self.cnt = 0


class Op:
    __slots__ = ("eng", "fn", "deps", "kind", "sig", "val", "res", "ndma")


class Prog:
    ENG = ("sync", "act", "pool", "dve", "pe")

    def __init__(self):
        self.ops = []
        self.dma_res = []

    def add(self, eng, fn, reads=(), writes=(), dma_res=None, ndma=0):
        i = len(self.ops)
        op = Op()
        op.eng, op.fn, op.kind = eng, fn, ("d" if dma_res is not None else "c")
        op.sig, op.val, op.res, op.ndma = False, 0, dma_res, ndma
        deps = {}
        for r in reads:
            if r.w is not None:
                deps[r.w] = "raw"
        for w in writes:
            if w.w is not None and w.w not in deps:
                deps[w.w] = "waw"
            for j in w.r:
                if j not in deps:
                    deps[j] = "war"
        for j in [j for j in deps if self.ops[j].fn is None]:
            typ = deps.pop(j)
            if self.ops[j].eng == eng:
                continue
            for j2 in self.ops[j].deps:
                if j2 not in deps:
                    deps[j2] = "raw"
        out = []
        latest = {}
        for j, typ in deps.items():
            oj = self.ops[j]
            if oj.kind == "c" and oj.eng == eng and op.kind == "c":
                if eng == "pe":
                    continue
                if typ == "waw" and eng != "pool":
                    continue
            if oj.kind == "c":
                if latest.get(oj.eng, -1) < j:
                    latest[oj.eng] = j
            else:
                out.append(j)
        out.extend(latest.values())
        op.deps = out
        for r in reads:
            r.r.append(i)
        for w in writes:
            w.w = i
            w.r = []
        if dma_res is not None and dma_res not in self.dma_res:
            self.dma_res.append(dma_res)
        self.ops.append(op)
        return i

    def barrier(self):
        last = {}
        dmas = []
        for idx, op in enumerate(self.ops):
            if op.fn is None:
                continue
            if op.kind == "c":
                last[op.eng] = idx
            elif idx >= getattr(self, "bar_start", 0):
                dmas.append(idx)
        for eng in self.ENG:
            i = self.add(eng, None)
            self.ops[i].deps = [j for e2, j in last.items() if (e2 != eng or eng == "pool")] + list(dmas)
        self.bar_start = len(self.ops)

    def emit(self, nc, stack):
        ops = self.ops
        for op in ops:
            for j in op.deps:
                ops[j].sig = True
        esem = {e: stack.enter_context(nc.semaphore("sem_" + e)) for e in self.ENG}
        for r in self.dma_res:
            r.sem = stack.enter_context(nc.semaphore("dsem_" + r.name))
            r.cnt = 0
        ecnt = {e: 0 for e in self.ENG}
        for op in ops:
            if op.kind == "d":
                op.res.cnt += 16 * op.ndma
                op.val = op.res.cnt
            elif op.sig:
                ecnt[op.eng] += 1
                op.val = ecnt[op.eng]
        self.counts = dict(ecnt)
        by_eng = {e: [op for op in ops if op.eng == e] for e in self.ENG}

        def run(ename, e):
            waited = {}
            for op in by_eng[ename]:
                for j in op.deps:
                    oj = ops[j]
                    sem = oj.res.sem if oj.kind == "d" else esem[oj.eng]
                    key = sem.name
                    if waited.get(key, 0) >= oj.val:
                        continue
                    waited[key] = oj.val
                    e.wait_ge(sem, oj.val)
                if op.fn is None:
                    continue
                r = op.fn(e)
                if op.kind == "d":
                    assert len(r) == op.ndma
                    for ins in r:
                        ins.then_inc(op.res.sem, 16)
                elif op.sig:
                    r.then_inc(esem[op.eng], 1)

        block = stack.enter_context(nc.Block())

        @block.sync
        def _(e):
            run("sync", e)

        @block.scalar
        def _(e):
            run("act", e)

        @block.gpsimd
        def _(e):
            run("pool", e)

        @block.vector
        def _(e):
            run("dve", e)

        @block.tensor
        def _(e):
            run("pe", e)


def bc(ap, axis, n):
    a = [list(x) for x in ap.ap]
    a.insert(axis, [0, n])
    return bass.AP(ap.tensor, ap.offset, a)


def t5_bucket(n):
    max_exact = 16
    nf = np.maximum(n, 1).astype(np.float32)
    large = max_exact + (np.log(nf / max_exact) / math.log(2048 / max_exact) * (32 - max_exact)).astype(np.int32)
    large = np.minimum(large, 31)
    return np.where(n < max_exact, n, large).astype(np.int32)


DILS = (1, 4, 16)


def bias_tables(rel_bias):
    k = np.arange(128)[:, None]
    q = np.arange(128)[None, :]
    out = np.full((3, 16, 128, 2, 128), NEG, np.float32)
    for g, dil in enumerate(DILS):
        st = q + 128 - k
        valid = st <= 128
        b = t5_bucket(np.clip(st, 0, 128) * dil)
        for h in range(16):
            out[g, h, :, 0, :] = np.where(valid, rel_bias[b, h], NEG)
        st = q - k
        valid = st >= 0
        b = t5_bucket(np.clip(st, 0, 128) * dil)
        for h in range(16):
            out[g, h, :, 1, :] = np.where(valid, rel_bias[b, h], NEG)
    out = out.reshape(3, 8, 2, 128, 2, 128).transpose(1, 3, 0, 2, 4, 5)
    return np.ascontiguousarray(out)


def build(layers=(0, 1), final=True, dbg=False):
    nc = bass.Bass("TRN2", target_bir_lowering=False)
    st = ExitStack()
    P = Prog()

    def dram(name, shape, kind="ExternalInput", dt=F32):
        return nc.dram_tensor(name, list(shape), dt, kind=kind).ap()

    def sb(name, shape, dt):
        return st.enter_context(nc.sbuf_tensor(name, list(shape), dt))

    xT_d = dram("xT", [NB, 128, 8, S])
    cT_d = dram("cT", [128, 8, NB])
    adaw_d = dram("ada_w", [2, 128, 8, 3072])
    adab_d = dram("ada_bT", [128, 2, 24])
    ng_d = dram("norm_gT", [128, 2, 8])
    fg_d = dram("final_gT", [128, 8])
    cst_d = dram("consts", [128, 512])
    if 0 in layers:
        gw_d = dram("gla_wh", [4, 3, 128, 8, 256])
        gwl_d = dram("gla_wlr", [128, 8, 16])
        gwa_d = dram("gla_walpha", [17, 512])
        ggn_d = dram("gla_gnT", [128, 2])
        gwo_d = dram("gla_wo", [128, 8, 1024])
    if 1 in layers:
        dw_d = dram("dsw_wh", [8, 3, 128, 8, 384])
        dwg_d = dram("dsw_wg", [8, 128, 8, 128])
        dwo_d = dram("dsw_wo", [128, 8, 1024])
        db_d = dram("dsw_bias", [8, 128, 1536])
    outT_d = dram("outT", [NB, 128, 8, S], kind="ExternalOutput")

    xT = sb("xT_sb", [128, 8, S], F32)
    hT = sb("hT_sb", [128, 8, S], BF16)
    ogT = sb("ogT_sb", [128, 8, S], BF16)
    wbuf = sb("wbuf", [128, 8 * 1280], BF16)
    cst = sb("cst", [128, 512], F32)
    cstb = sb("cstb", [128, 512], BF16)
    c_sb = sb("c_sb", [128, 8, NB], F32)
    cact = sb("cact", [128, 8, NB], F32)
    adab = sb("adab", [128, 2, 24], F32)
    ngT = sb("ngT", [128, 2, 8], F32)
    fgT = sb("fgT", [128, 8], F32)
    modT = sb("modT", [128, 2, 24, NB], F32)
    gsT = sb("gsT", [128, 2, 8, NB], F32)
    ones_f = sb("ones_f", [128, 128], BF16)
    ones_hb = sb("ones_hb", [128, 128], BF16)

    ARENA16 = 24832 + 3072 - 1792 - 64 + 2048
    arena = sb("arena", [128, ARENA16], BF16)
    aoff = [0]

    def carve(shape, dt):
        n = 1
        for d_ in shape[1:]:
            n *= d_
        n16 = n * (2 if dt == F32 else 1)
        n16 = (n16 + 15) // 16 * 16
        assert aoff[0] + n16 <= ARENA16, (aoff[0], n16)
        v = arena[:, aoff[0]:aoff[0] + n16]
        aoff[0] += n16
        if dt == F32:
            v = v.bitcast(F32)
        v = v[:, 0:n]
        if len(shape) == 3:
            v = v.rearrange("p (a b) -> p a b", a=shape[1])
        return v

    R = {}

    def res(name):
        if name not in R:
            R[name] = Res(name)
        return R[name]

    wg = [res("wg%d" % i) for i in range(20)]
    xr2 = [[res("x%d_%d" % (t_, c)) for c in range(8)] for t_ in range(4)]
    xr = [r_ for t_ in xr2 for r_ in t_]
    hr = [res("h%d" % c) for c in range(8)]
    ogr = [res("og%d" % c) for c in range(8)]

    ps = [st.enter_context(nc.psum_tensor("ps%d" % i, [128, 512], F32)) for i in range(8)]
    pr = [res("psum%d" % i) for i in range(8)]
    acc_rot = [0]

    def acc_bank():
        acc_rot[0] ^= 1
        return acc_rot[0]

    ident = cstb[:, 0:128]
    ones64 = cstb[:, 384:448]
    U2 = cst[:, 128:256]
    maskT = cst[:, 256:384]

    P.add("sync", lambda e: [e.dma_start(out=cst[:], in_=cst_d),
                             e.dma_start(out=c_sb[:], in_=cT_d),
                             e.dma_start(out=adab[:], in_=adab_d),
                             e.dma_start(out=ngT[:], in_=ng_d),
                             e.dma_start(out=fgT[:], in_=fg_d)],
          writes=[res("cst"), res("smallin")], dma_res=res("cst"), ndma=5)
    P.add("dve", lambda e: e.tensor_copy(out=cstb[:], in_=cst[:]), reads=[res("cst")], writes=[res("cstb")])
    P.add("pool", lambda e: e.memset(ones_f[:], 1.0 / 1024.0), writes=[res("ones_f")])
    P.add("pool", lambda e: e.memset(ones_hb[:], 1.0 / 256.0), writes=[res("cstb")])
    P.add("act", lambda e: e.activation(out=cact[:], in_=c_sb[:], func=AF.Silu),
          reads=[res("cst")], writes=[res("cact")])

    ogT_f = ogT[:].rearrange("p c t -> p (c t)").bitcast(F32)
    mrow = arena[0:2, 20480:22528].bitcast(F32).rearrange("p (s n) -> p s n", s=2)
    def mod_piece(l, piece):
        pm = ps[3]
        half = piece % 2
        stg = ogT_f[:, half * 4096:(half + 1) * 4096].rearrange("p (k n) -> p k n", k=8)
        sres = ogr[half * 4:(half + 1) * 4]
        P.add("sync", lambda e, stg=stg, l=l, piece=piece: [
            e.dma_start(out=stg, in_=adaw_d[l, :, :, piece * 512:(piece + 1) * 512])],
            writes=sres, dma_res=res("adaw%d" % half), ndma=1)
        for k in range(8):
            P.add("pe", lambda e, stg=stg, k=k: e.matmul(
                ps[2][0:NB, :], lhsT=cact[:, k, :], rhs=stg[:, k, :], start=(k == 0), stop=(k == 7)),
                reads=sres + [res("cact")], writes=[pr[2]])
        P.add("act", lambda e, half=half: e.activation(out=mrow[:, half, :], in_=ps[2][0:NB, :], func=AF.Identity),
              reads=[pr[2]], writes=[res("mrow%d" % half)])
        for fi in range(4):
            f = piece * 4 + fi
            P.add("pe", lambda e, f=f, fi=fi, half=half, pm=pm: e.matmul(
                pm[:, f * NB:(f + 1) * NB], lhsT=mrow[:, half, fi * 128:(fi + 1) * 128], rhs=cst[0:NB, 0:NB],
                start=True, stop=True),
                reads=[res("mrow%d" % half), res("cst")], writes=[pr[3]])

    def mod_finish(l):
        pm = ps[3]
        P.add("dve", lambda e, l=l, pm=pm: e.tensor_tensor(
            out=modT[:, l, :, :], in0=pm[:, 0:24 * NB].rearrange("p (f b) -> p f b", b=NB),
            in1=bc(adab[:, l, :], 2, NB), op=ALU.add),
            reads=[pr[3], res("smallin")], writes=[res("mod%d" % l)])
        P.add("dve", lambda e, l=l: e.tensor_scalar(
            out=gsT[:, l, :, :], in0=modT[:, l, 8:16, :], scalar1=1.0, scalar2=None, op0=ALU.add),
            reads=[res("mod%d" % l)], writes=[res("gs%d" % l)])
        P.add("dve", lambda e, l=l: e.tensor_tensor(
            out=gsT[:, l, :, :], in0=gsT[:, l, :, :], in1=bc(ngT[:, l, :], 2, NB), op=ALU.mult),
            reads=[res("gs%d" % l), res("smallin")], writes=[res("gs%d" % l)])

    for piece in range(6):
        mod_piece(layers[0], piece)
    mod_finish(layers[0])
    deferred_mod = [(layers[1], p) for p in range(6)] if len(layers) > 1 else []

    st_ops = []
    cur_alias = {}

    def sr(name):
        return [res(name)]

    hp_sq = [arena[:, i * 4096:i * 4096 + 2048].rearrange("p (c t) -> p c t", c=4) for i in range(2)]
    hp_t = [arena[:, 8192 + i * 4096:8192 + (i + 1) * 4096].bitcast(F32).rearrange("p (c t) -> p c t", c=4) for i in range(2)]
    hp_rstd = [arena[:, 16384 + i * 1024:16384 + (i + 1) * 1024].bitcast(F32) for i in range(2)]

    def stats(tile):
        tsl = slice(tile * 512, (tile + 1) * 512)
        rs = hp_rstd[tile % 2]
        rname = "hp_rstd%d" % (tile % 2)
        for hf in range(2):
            P.add("act", lambda e, hf=hf: e.activation(out=hp_sq[hf], in_=xT[:, 4 * hf:4 * hf + 4, tsl], func=AF.Square),
                  reads=xr2[tile][4 * hf:4 * hf + 4], writes=[res("hp_sq%d" % hf)])
            for ci in range(4):
                c = 4 * hf + ci
                P.add("pe", lambda e, hf=hf, ci=ci, c=c: e.matmul(ps[2][:, :], lhsT=ones_f[:], rhs=hp_sq[hf][:, ci, :],
                                                                start=(c == 0), stop=(c == 7)),
                      reads=[res("hp_sq%d" % hf), res("ones_f")], writes=[pr[2]])
        P.add("act", lambda e, rs=rs: e.activation(out=rs, in_=ps[2][:, :], func=AF.Ln, bias=EPS),
              reads=[pr[2]], writes=[res(rname)])
        P.add("act", lambda e, rs=rs: e.activation(out=rs, in_=rs, func=AF.Exp, scale=-0.5),
              reads=[res(rname)], writes=[res(rname)])
        return rs, res(rname)

    def apply_h(l, b, tile, rs, rres):
            tsl = slice(tile * 512, (tile + 1) * 512)
            for hf in range(2):
                P.add("dve", lambda e, hf=hf, tsl=tsl, rs=rs: e.tensor_tensor(
                    out=hp_t[hf], in0=xT[:, 4 * hf:4 * hf + 4, tsl], in1=bc(rs, 1, 4), op=ALU.mult),
                    reads=xr2[tile][4 * hf:4 * hf + 4] + [rres], writes=sr("hp_t%d" % hf))
                for ci in range(4):
                    c = 4 * hf + ci
                    if ci % 2 == 0:
                        P.add("act", lambda e, c=c, ci=ci, hf=hf, tsl=tsl: e.activation(
                            out=hT[:, c, tsl], in_=hp_t[hf][:, ci, :], func=AF.Identity,
                            bias=modT[:, l, c, b:b + 1], scale=gsT[:, l, c, b:b + 1]),
                            reads=sr("hp_t%d" % hf) + [res("mod%d" % l), res("gs%d" % l)], writes=[hr[c]])
                    else:
                        P.add("dve", lambda e, c=c, ci=ci, hf=hf, tsl=tsl: e.tensor_scalar(
                            out=hT[:, c, tsl], in0=hp_t[hf][:, ci, :], scalar1=gsT[:, l, c, b:b + 1],
                            scalar2=modT[:, l, c, b:b + 1], op0=ALU.mult, op1=ALU.add),
                            reads=sr("hp_t%d" % hf) + [res("mod%d" % l), res("gs%d" % l)], writes=[hr[c]])

    def hprep(l, b):
        if b == 0 or 1 not in layers:
            P.barrier()
        else:
            P.add("act", None, writes=[res("hp_sq0"), res("accN")])
            P.add("act", None, writes=[res("hp_sq1"), res("accD")] + [res("fs_sq%d" % i_) for i_ in range(5)])
            P.add("dve", None, writes=[res("hp_t0"), res("qT1"), res("kT1")])
            P.add("dve", None, writes=[res("hp_t1"), res("kT1"), res("vT1")])
            P.add("act", None, writes=[res("hp_rstd0"), res("hp_rstd1"), res("vtm1")])
        if deferred_mod:
            mod_piece(*deferred_mod.pop(0))
        nxt_stats = stats(0)
        for tile in range(4):
            rs, rres = nxt_stats
            if tile + 1 < 4:
                if deferred_mod:
                    mod_piece(*deferred_mod.pop(0))
                nxt_stats = stats(tile + 1)
            apply_h(l, b, tile, rs, rres)
        if deferred_mod:
            l2 = deferred_mod[0][0]
            while deferred_mod:
                mod_piece(*deferred_mod.pop(0))
            mod_finish(l2)

    fs_sq = [arena[:, 4096 + i * 512:4096 + (i + 1) * 512] for i in range(5)]
    hT_f = hT[:].rearrange("p c t -> p (c t)").bitcast(F32)

    def apply_final(b, tile, rs, rres, stg, sres):
        tsl = slice(tile * 512, (tile + 1) * 512)
        for hf in range(2):
            P.add("dve", lambda e, hf=hf, tsl=tsl, rs=rs: e.tensor_tensor(
                out=hp_t[hf], in0=xT[:, 4 * hf:4 * hf + 4, tsl], in1=bc(rs, 1, 4), op=ALU.mult),
                reads=xr2[tile][4 * hf:4 * hf + 4] + [rres], writes=sr("hp_t%d" % hf))
            for ci in range(4):
                c = 4 * hf + ci
                if ci % 2 == 0:
                    P.add("act", lambda e, c=c, ci=ci, hf=hf, stg=stg: e.activation(
                        out=stg[:, c, :], in_=hp_t[hf][:, ci, :], func=AF.Identity, scale=fgT[:, c:c + 1]),
                        reads=sr("hp_t%d" % hf) + [res("smallin")], writes=sres)
                else:
                    P.add("dve", lambda e, c=c, ci=ci, hf=hf, stg=stg: e.tensor_scalar(
                        out=stg[:, c, :], in0=hp_t[hf][:, ci, :], scalar1=fgT[:, c:c + 1], scalar2=None, op0=ALU.mult),
                        reads=sr("hp_t%d" % hf) + [res("smallin")], writes=sres)
        st_ops.append(P.add("sync", lambda e, tsl=tsl, stg=stg: [e.dma_start(out=outT_d[b, :, :, tsl], in_=stg)],
                            reads=sres, dma_res=res("outst%d" % (tile % 2)), ndma=1))

    wo_view = wbuf[:, 0:8192].rearrange("p (e n) -> p e n", e=8)

    def load_wout(w_d, l, chunks):
        gr = []
        for e_ in chunks:
            gr += wg[2 * e_:2 * e_ + 2]
        P.add("pool", lambda e, chunks=tuple(chunks): [e.dma_start(out=wo_view[:, e_, :], in_=w_d[:, e_, :]) for e_ in chunks],
              writes=gr, dma_res=res("wo%d_%d" % (l, len(chunks))), ndma=len(chunks))
        return wo_view

    op_rot = [0]

    def outproj(l, b, wo, nxt=None, alias=None):
        LAG = 4
        cur_alias.clear()
        if nxt is not None:
            if alias is None:
                P.barrier()
            else:
                for eng_, names in (("act", ["fs_sq0", "fs_sq1", "fs_sq2", "fs_sq3", "fs_sq4", "hp_rstd0", "hp_rstd1"]),
                                    ("dve", ["hp_t0", "hp_t1"])):
                    for nm in names:
                        al = [x for pre, lst in alias.items() if nm.startswith(pre) for x in lst]
                        P.add(eng_, None, writes=[res(nm)] + [res(x) for x in al])
        for tile in range(4):
            tsl = slice(tile * 512, (tile + 1) * 512)

            def stat_mm(n):
                P.add("pe", lambda e, n=n: e.matmul(ps[2][:, :], lhsT=ones_f[:], rhs=fs_sq[n % 5],
                                                    start=(n == 0), stop=(n == 7)),
                      reads=sr("fs_sq%d" % (n % 5)) + [res("ones_f")], writes=[pr[2]])

            for n in range(8):
                op_rot[0] = (op_rot[0] + 1) % 7
                bk = (0, 1, 3, 4, 5, 6, 7)[op_rot[0]]
                for e_ in range(8):
                    P.add("pe", lambda e, e_=e_, n=n, bk=bk, tsl=tsl: e.matmul(
                        ps[bk][:, :], lhsT=wo[:, e_, n * 128:(n + 1) * 128], rhs=ogT[:, e_, tsl],
                        start=(e_ == 0), stop=(e_ == 7)),
                        reads=wg[0:16] + [ogr[e_]], writes=[pr[bk]])
                P.add("dve", lambda e, n=n, bk=bk, tsl=tsl: e.scalar_tensor_tensor(
                    out=xT[:, n, tsl], in0=ps[bk][:, :], scalar=modT[:, l, 16 + n, b:b + 1], in1=xT[:, n, tsl],
                    op0=ALU.mult, op1=ALU.add),
                    reads=[pr[bk], xr2[tile][n], res("mod%d" % l)], writes=[xr2[tile][n]])
                if nxt is not None:
                    P.add("act", lambda e, n=n, tsl=tsl: e.activation(out=fs_sq[n % 5], in_=xT[:, n, tsl], func=AF.Square),
                          reads=[xr2[tile][n]], writes=sr("fs_sq%d" % (n % 5)))
                    if n >= LAG:
                        stat_mm(n - LAG)
            if nxt is not None:
                for n in range(8 - LAG, 8):
                    stat_mm(n)
                rs = hp_rstd[tile % 2]
                rname = "hp_rstd%d" % (tile % 2)
                P.add("act", lambda e, rs=rs: e.activation(out=rs, in_=ps[2][:, :], func=AF.Ln, bias=EPS),
                      reads=[pr[2]], writes=sr(rname))
                P.add("act", lambda e, rs=rs: e.activation(out=rs, in_=rs, func=AF.Exp, scale=-0.5),
                      reads=sr(rname), writes=sr(rname))
                if nxt[0] == "hprep":
                    apply_h(nxt[1], b, tile, rs, res(rname))
                else:
                    half = tile % 2
                    stg = hT_f[:, half * 4096:(half + 1) * 4096].rearrange("p (k n) -> p k n", k=8)
                    apply_final(b, tile, rs, res(rname), stg, hr[half * 4:(half + 1) * 4])
        cur_alias.clear()

    def proj_fm(wk, col0, tile, bk, wres):
        tsl = slice(tile * 512, (tile + 1) * 512)
        for k in range(8):
            P.add("pe", lambda e, k=k: e.matmul(ps[bk][:, :], lhsT=wk[:, k, col0:col0 + 128], rhs=hT[:, k, tsl],
                                                start=(k == 0), stop=(k == 7)),
                  reads=wres + [hr[k]], writes=[pr[bk]])

    if 0 in layers:
        aoff[0] = 0
        ez = carve([128, 512], F32)
        lap = carve([128, 4, 128], F32)
        E1 = carve([128, 512], F32)
        E2 = carve([128, 512], F32)
        wst0 = arena[:, 0:4096].bitcast(F32).rearrange("p (s n) -> p s n", s=2)
        E3 = ez
        glrT = carve([128, 512], F32)
        qTs = carve([128, S], BF16)
        kTs = carve([128, S], BF16)
        kkvT = carve([128, S], BF16)
        vtm = carve([128, 16, 256], BF16)
        ktmA = carve([128, 16, 128], BF16)
        ktmB = carve([128, 16, 128], BF16)
        ract = carve([128, 4, 256], BF16)
        dec = carve([128, 32], F32)
        ATs = carve([128, 4, 128], BF16)
        Sall = arena[:, 1024:3584].bitcast(F32).rearrange("p (s e) -> p s e", s=5)
        Sbf = carve([128, 8, 256], BF16)
        osq = ez[:, 0:256].bitcast(BF16)
        rso = carve([128, 2, 128], F32)
        otmp = carve([128, 1024], F32)
        walp = carve([128, 512], F32)
        wlr = carve([128, 8, 16], BF16)
        gnT = carve([128, 2], F32)

    def gla_init():
        P.barrier()
        P.add("pool", lambda e: [e.dma_start(out=wlr, in_=gwl_d)], writes=[res("wlr")], dma_res=res("wlr"), ndma=1)
        P.add("sync", lambda e: [e.dma_start(out=walp[0:17, :], in_=gwa_d), e.dma_start(out=gnT, in_=ggn_d)],
              writes=[res("walp"), res("gnT")], dma_res=res("walp"), ndma=2)
        P.add("pool", lambda e: e.memset(glrT[0:32, :], 1.0), writes=[res("glrT")])
        P.add("pool", lambda e: e.memset(ktmA[:], 0.0), writes=[res("ktm")])
        P.add("pool", lambda e: e.memset(ktmB[:], 0.0), writes=[res("ktm")])

    def gla_layer(b):
        l = 0
        hprep(l, b)
        gla_init()
        def slot0(h, part):
            i = (3 * h + part) % 5
            return wbuf[:, i * 2048:(i + 1) * 2048].rearrange("p (k n) -> p k n", k=8), wg[4 * i:4 * i + 4], i

        def wdma0(h, part):
            ap, rs, i = slot0(h, part)
            P.add("pool", lambda e, ap=ap, h=h, part=part: [e.dma_start(out=ap, in_=gw_d[h, part])],
                  writes=rs, dma_res=res("w0slot%d" % i), ndma=1)

        for part in range(3):
            wdma0(0, part)
        for h in range(4):
            if h + 1 < 4:
                wdma0(h + 1, 0)
                wdma0(h + 1, 1)
            wA, rA, _ = slot0(h, 0)
            wV, rV, _ = slot0(h, 1)
            wR, rR, _ = slot0(h, 2)
            def emit_glr(tile):
                tsl = slice(tile * 512, (tile + 1) * 512)
                bk = acc_bank()
                for k in range(8):
                    P.add("pe", lambda e, k=k, bk=bk, tsl=tsl: e.matmul(
                        ps[bk][0:16, :], lhsT=wlr[:, k, :], rhs=hT[:, k, tsl], start=(k == 0), stop=(k == 7)),
                        reads=[res("wlr"), hr[k]], writes=[pr[bk]])
                P.add("dve", lambda e, bk=bk: e.tensor_copy(out=glrT[0:16, :], in_=ps[bk][0:16, :]),
                      reads=[pr[bk]], writes=[res("glrT")])

            def emit_z(tile, h=h):
                for s4 in range(4):
                    P.add("pe", lambda e, s4=s4, h=h: e.matmul(
                        ps[3][:, s4 * 128:(s4 + 1) * 128], lhsT=glrT[0:17, s4 * 128:(s4 + 1) * 128],
                        rhs=walp[0:17, h * 128:(h + 1) * 128], start=True, stop=True),
                        reads=[res("glrT"), res("walp")], writes=[pr[3]])
                P.add("act", lambda e: e.activation(out=ez[:], in_=ps[3][:, :], func=AF.Exp, scale=-1.0),
                      reads=[pr[3]], writes=[res("ez")])
                P.add("act", lambda e: e.activation(out=lap[:].rearrange("p s d -> p (s d)"), in_=ez[:], func=AF.Ln, bias=1.0),
                      reads=[res("ez")], writes=[res("lap")])

            def emit_bT(tile):
                for s4 in range(4):
                    P.add("pe", lambda e, s4=s4: e.matmul(
                        ps[4][:, s4 * 128:(s4 + 1) * 128], lhsT=lap[:, s4, :], rhs=U2, start=True, stop=True),
                        reads=[res("lap"), res("cst")], writes=[pr[4]])
                P.add("act", lambda e: e.activation(out=E1[:], in_=ps[4][:, :], func=AF.Exp, bias=math.log(128 ** -0.5)),
                      reads=[pr[4]], writes=[res("E1")])
                P.add("act", lambda e: e.activation(out=E2[:], in_=ps[4][:, :], func=AF.Exp, scale=-1.0),
                      reads=[pr[4]], writes=[res("E2")])
                P.add("act", lambda e, tile=tile: e.activation(
                    out=dec[:, tile * 8:(tile + 1) * 8],
                    in_=ps[4][:, :].rearrange("p (c j) -> p c j", j=64)[:, :, 63], func=AF.Exp),
                    reads=[pr[4]], writes=[res("dec")])
                P.add("dve", lambda e, tile=tile: e.tensor_tensor(
                    out=E3[:, :].rearrange("p (c j) -> p c j", j=64), in0=E2[:].rearrange("p (c j) -> p c j", j=64),
                    in1=bc(dec[:, tile * 8:(tile + 1) * 8], 2, 64), op=ALU.mult),
                    reads=[res("E2"), res("dec")], writes=[res("ez")])

            def emit_qk(tile, wA=wA, rA=rA):
                tsl = slice(tile * 512, (tile + 1) * 512)
                bk = acc_bank()
                proj_fm(wA, 0, tile, bk, rA)
                P.add("dve", lambda e, bk=bk, tsl=tsl: e.tensor_tensor(out=qTs[:, tsl], in0=ps[bk][:, :], in1=E1[:], op=ALU.mult),
                      reads=[pr[bk], res("E1")], writes=[res("qTs")])
                bk = acc_bank()
                proj_fm(wA, 128, tile, bk, rA)
                P.add("dve", lambda e, bk=bk, tsl=tsl: e.tensor_tensor(out=kTs[:, tsl], in0=ps[bk][:, :], in1=E2[:], op=ALU.mult),
                      reads=[pr[bk], res("E2")], writes=[res("kTs")])
                P.add("dve", lambda e, bk=bk, tsl=tsl: e.tensor_tensor(out=kkvT[:, tsl], in0=ps[bk][:, :], in1=E3[:], op=ALU.mult),
                      reads=[pr[bk], res("ez")], writes=[res("kkvT")])

            def emit_vhalf(tile, sp, wV=wV, rV=rV):
                bk = acc_bank()
                for s2 in range(2):
                    s_ = tile * 4 + sp * 2 + s2
                    for k in range(8):
                        P.add("pe", lambda e, k=k, s_=s_, s2=s2, bk=bk, wV=wV: e.matmul(
                            ps[bk][:, s2 * 256:(s2 + 1) * 256], lhsT=hT[:, k, s_ * 128:(s_ + 1) * 128],
                            rhs=wV[:, k, 0:256], start=(k == 0), stop=(k == 7)),
                            reads=rV + [hr[k]], writes=[pr[bk]])
                s0 = tile * 4 + sp * 2
                P.add("act", lambda e, bk=bk, s0=s0: e.activation(
                    out=vtm[:, s0:s0 + 2, :].rearrange("p s d -> p (s d)"), in_=ps[bk][:, :], func=AF.Identity),
                    reads=[pr[bk]], writes=[res("vtm")])

            def emit_tr(tile):
                pv = ps[5][:, :].bitcast(BF16)
                for s4 in range(4):
                    s_ = tile * 4 + s4
                    P.add("pe", lambda e, s_=s_, s4=s4, pv=pv: e.transpose(
                        pv[:, s4 * 128:(s4 + 1) * 128], kkvT[:, s_ * 128:(s_ + 1) * 128], ident),
                        reads=[res("kkvT"), res("cstb")], writes=[pr[5]])
                s0 = tile * 4
                P.add("dve", lambda e, pv=pv, s0=s0: e.tensor_copy(
                    out=ktmA[0:64, s0:s0 + 4, :], in_=pv[0:64, 0:512].rearrange("p (s d) -> p s d", s=4)),
                    reads=[pr[5]], writes=[res("ktm")])
                P.add("dve", lambda e, pv=pv, s0=s0: e.tensor_copy(
                    out=ktmB[64:128, s0:s0 + 4, :], in_=pv[64:128, 0:512].rearrange("p (s d) -> p s d", s=4)),
                    reads=[pr[5]], writes=[res("ktm")])

            for tile in range(4):
                emit_glr(tile)
                if tile > 0:
                    emit_tr(tile - 1)
                emit_vhalf(tile, 0)
                emit_z(tile)
                emit_vhalf(tile, 1)
                emit_bT(tile)
                emit_qk(tile)
            emit_tr(3)
            if h + 1 < 4:
                wdma0(h + 1, 2)
            else:
                load_wout(gwo_d, 0, [0, 1, 4, 5, 6, 7])
            SN = [res("lap"), res("E1"), res("E2")]
            P.add("pool", lambda e: e.memset(Sall[:, 0, :], 0.0), writes=SN)

            def emit_A(j):
                for sub in range(2):
                    s_ = 2 * j + sub
                    ssl = slice(s_ * 128, (s_ + 1) * 128)
                    P.add("pe", lambda e, ssl=ssl, sub=sub: e.matmul(
                        ps[5][:, sub * 128:(sub + 1) * 128], lhsT=kTs[:, ssl], rhs=qTs[:, ssl], start=True, stop=True),
                        reads=[res("kTs"), res("qTs")], writes=[pr[5]])

            def emit_mask(j):
                ab = j % 2
                P.add("dve", lambda e, ab=ab: e.tensor_tensor(
                    out=ATs[:, ab * 2:ab * 2 + 2, :], in0=ps[5][:, 0:256].rearrange("p (s t) -> p s t", s=2),
                    in1=bc(maskT, 1, 2), op=ALU.mult),
                    reads=[pr[5], res("cst")], writes=[res("ATs%d" % ab)])

            def emit_kv(j):
                for i in range(4):
                    c = 4 * j + i
                    s_ = c // 2
                    kb = 7 if i < 2 else 3
                    ktm_c = ktmA if c % 2 == 0 else ktmB
                    P.add("pe", lambda e, ktm_c=ktm_c, s_=s_, kb=kb, i=i: e.matmul(
                        ps[kb][:, (i % 2) * 256:(i % 2 + 1) * 256], lhsT=ktm_c[:, s_, :], rhs=vtm[:, s_, :], start=True, stop=True),
                        reads=[res("ktm"), res("vtm")], writes=[pr[kb]])

            def emit_scan(j):
                for i in range(4):
                    c = 4 * j + i
                    kb = 7 if i < 2 else 3
                    si, so = (i, i + 1) if j % 2 == 0 else (4 - i, 3 - i)
                    P.add("dve", lambda e, c=c, kb=kb, i=i, si=si, so=so: e.scalar_tensor_tensor(
                        out=Sall[:, so, :], in0=Sall[:, si, :], scalar=dec[:, c:c + 1],
                        in1=ps[kb][:, (i % 2) * 256:(i % 2 + 1) * 256], op0=ALU.mult, op1=ALU.add),
                        reads=SN + [pr[kb], res("dec")], writes=SN)

            def emit_copies(j):
                lo = 0 if j % 2 == 0 else 1
                P.add("dve", lambda e, j=j, lo=lo: e.tensor_copy(
                    out=Sbf[:, (j % 2) * 4:(j % 2) * 4 + 4, :], in_=Sall[:, lo:lo + 4, :]),
                    reads=SN, writes=[res("Sbf%d" % (j % 2))])

            def emit_rproj(j):
                t0 = j * 256
                bk = acc_bank()
                for half in range(2):
                    for k in range(8):
                        P.add("pe", lambda e, k=k, half=half, bk=bk, t0=t0, wR=wR: e.matmul(
                            ps[bk][:, half * 256:(half + 1) * 256], lhsT=wR[:, k, half * 128:(half + 1) * 128],
                            rhs=hT[:, k, t0:t0 + 256], start=(k == 0), stop=(k == 7)),
                            reads=rR + [hr[k]], writes=[pr[bk]])
                rb = j % 2
                sgt = otmp[:, rb * 512:(rb + 1) * 512]
                P.add("act", lambda e, bk=bk, sgt=sgt: e.activation(out=sgt, in_=ps[bk][:, :], func=AF.Exp, scale=-1.0),
                      reads=[pr[bk]], writes=[res("otmp%d" % rb)])
                P.add("act", lambda e, sgt=sgt: e.activation(out=sgt, in_=sgt, func=AF.Ln, bias=1.0),
                      reads=[res("otmp%d" % rb)], writes=[res("otmp%d" % rb)])
                P.add("act", lambda e, sgt=sgt: e.activation(out=sgt, in_=sgt, func=AF.Exp, scale=-1.0),
                      reads=[res("otmp%d" % rb)], writes=[res("otmp%d" % rb)])
                for half in range(2):
                    P.add("dve", lambda e, bk=bk, rb=rb, sgt=sgt, half=half: e.scalar_tensor_tensor(
                        out=ract[:, 2 * rb + half, :], in0=ps[bk][:, half * 256:(half + 1) * 256], scalar=gnT[:, half:half + 1],
                        in1=sgt[:, half * 256:(half + 1) * 256], op0=ALU.mult, op1=ALU.mult),
                        reads=[pr[bk], res("otmp%d" % rb), res("gnT")], writes=[res("ract%d" % rb)])

            def emit_oT(j, j_=None):
                ab = j % 2
                ob = 6 if j % 2 == 0 else 4
                first = True
                for sub in range(2):
                    s_ = 2 * j + sub
                    for half in range(2):
                        ocol = (sub * 2 + half) * 128
                        P.add("pe", lambda e, half=half, ocol=ocol, ab=ab, sub=sub, s_=s_, first=first, ob=ob: e.matmul(
                            ps[ob][:, ocol:ocol + 128], lhsT=vtm[:, s_, half * 128:(half + 1) * 128],
                            rhs=ATs[:, ab * 2 + sub, :], start=first, stop=False, skip_group_check=True),
                            reads=[res("vtm"), res("ATs%d" % ab)], writes=[pr[ob]])
                        first = False
                        for cc in range(2):
                            c = 2 * s_ + cc
                            i = c - 4 * j
                            if c == 0:
                                continue
                            P.add("pe", lambda e, half=half, ocol=ocol, cc=cc, c=c, i=i, ob=ob: e.matmul(
                                ps[ob][:, ocol + cc * 64:ocol + (cc + 1) * 64], lhsT=Sbf[:, (j % 2) * 4 + (i if j % 2 == 0 else 3 - i), half * 128:(half + 1) * 128],
                                rhs=qTs[:, c * 64:(c + 1) * 64], start=False, stop=(cc == 1), skip_group_check=True),
                                reads=[res("Sbf%d" % (j % 2)), res("qTs")], writes=[pr[ob]])

            def emit_post1(j):
                ob = 6 if j % 2 == 0 else 4
                P.add("act", lambda e, ob=ob: e.activation(out=osq[:], in_=ps[ob][:, :], func=AF.Square),
                      reads=[pr[ob]], writes=[res("ez")])

            def emit_post2(j):
                t0 = j * 256
                ob = 6 if j % 2 == 0 else 4
                for sb_ in range(2):
                    for half in range(2):
                        col = (sb_ * 2 + half) * 128
                        P.add("pe", lambda e, sb_=sb_, half=half, col=col: e.matmul(
                            ps[2][:, sb_ * 128:(sb_ + 1) * 128], lhsT=ones_hb[:], rhs=osq[:, col:col + 128],
                            start=(half == 0), stop=(half == 1)),
                            reads=[res("ez"), res("cstb")], writes=[pr[2]])
                P.add("act", lambda e: e.activation(out=rso[:].rearrange("p s t -> p (s t)"), in_=ps[2][:, 0:256], func=AF.Ln, bias=EPS),
                      reads=[pr[2]], writes=[res("rso")])
                P.add("act", lambda e: e.activation(out=rso[:].rearrange("p s t -> p (s t)"), in_=rso[:].rearrange("p s t -> p (s t)"),
                                                    func=AF.Exp, scale=-0.5),
                      reads=[res("rso")], writes=[res("rso")])
                rb = j % 2
                ot = otmp[:, rb * 512:(rb + 1) * 512]
                P.add("dve", lambda e, ob=ob, ot=ot: e.tensor_tensor(
                    out=ot.rearrange("p (s h t) -> p s h t", s=2, h=2),
                    in0=ps[ob][:, :].rearrange("p (s h t) -> p s h t", s=2, h=2),
                    in1=bc(rso[:], 2, 2), op=ALU.mult),
                    reads=[pr[ob], res("rso")], writes=[res("otmp%d" % rb)])
                for half in range(2):
                    P.add("pool", lambda e, t0=t0, h=h, rb=rb, ot=ot, half=half: e.tensor_tensor(
                        out=ogT[:, 2 * h + half, t0:t0 + 256].rearrange("p (s t) -> p s t", s=2),
                        in0=ot.rearrange("p (s h t) -> p s h t", s=2, h=2)[:, :, half, :],
                        in1=ract[:, 2 * rb + half, :].rearrange("p (s t) -> p s t", s=2), op=ALU.mult),
                        reads=[res("otmp%d" % rb), res("ract%d" % rb)], writes=[ogr[2 * h + half]])

            emit_A(0)
            emit_kv(0)
            emit_scan(0)
            emit_copies(0)
            emit_mask(0)
            for j in range(8):
                if j + 1 < 8:
                    emit_A(j + 1)
                    emit_mask(j + 1)
                    emit_kv(j + 1)
                    emit_scan(j + 1)
                    emit_copies(j + 1)
                emit_rproj(j)
                if j > 0:
                    emit_post2(j - 1)
                emit_oT(j)
                emit_post1(j)
            emit_post2(7)
        wo = load_wout(gwo_d, 0, [2, 3])
        outproj(0, b, wo, nxt=(("hprep", 1) if 1 in layers else None),
                alias={"fs_sq": ["glrT", "qTs"], "hp_t": ["kTs", "kkvT", "vtm", "ktm"], "hp_rstd": ["ktm"]})

    if 1 in layers:
        aoff[0] = 0
        accN = carve([128, S], F32)
        wst1 = arena[:, 0:4096].bitcast(F32).rearrange("p (s n) -> p s n", s=2)
        accD = carve([128, S], F32)
        qT1 = carve([128, S], BF16)
        kTA = carve([128, S], BF16)
        kTB = carve([128, S], BF16)
        vT1 = carve([128, S], BF16)
        vtm1 = carve([128, 16, 128], BF16)
        sg = carve([128, S], BF16)
        bias8 = carve([128, 1536], BF16)
        PT = carve([128, 2, 512], BF16)

    def dsw_init():
        P.barrier()
        P.add("pool", lambda e: e.memset(kTA[:], 0.0), writes=[res("kT1")])
        P.add("pool", lambda e: e.memset(kTB[:], 0.0), writes=[res("kT1")])

    def perm_views(dst, tile, dil, psrc):
        if dil == 1:
            return dst[:, tile * 512:(tile + 1) * 512], psrc
        L = S // dil
        n_i = 512 // dil
        o = dst[:, :].rearrange("p (r i) -> p r i", r=dil)[:, :, tile * n_i:(tile + 1) * n_i]
        i_ = psrc.rearrange("p (i r) -> p r i", r=dil)
        return o, i_

    def dsw_layer(b):
        l = 1
        if 0 not in layers:
            hprep(l, b)
        dsw_init()
        bv = bias8[:, :].rearrange("p (g h m q) -> p g h m q", g=3, h=2, m=2)
        wgate = wbuf[:, 9216:10240].rearrange("p (k n) -> p k n", k=8)
        rgate = wg[18:20]

        def slot1(u):
            j = u % 3
            return wbuf[:, j * 3072:(j + 1) * 3072].rearrange("p (k n) -> p k n", k=8), wg[6 * j:6 * j + 6], j

        def wdma1(u):
            ap, rs, j = slot1(u)
            P.add("pool", lambda e, ap=ap, u=u: [e.dma_start(out=ap, in_=dw_d[u // 3, u % 3])],
                  writes=rs, dma_res=res("w1slot%d" % j), ndma=1)

        def gdma(hp):
            P.add("pool", lambda e, hp=hp: [e.dma_start(out=wgate, in_=dwg_d[hp])],
                  writes=rgate, dma_res=res("w1gate"), ndma=1)

        gdma(0)
        wdma1(0)
        wdma1(1)
        def emit_combine(hp, t):
            ts_ = slice(t * 512, (t + 1) * 512)
            P.add("act", lambda e, ts_=ts_: e.activation(out=accD[:, ts_], in_=accD[:, ts_], func=AF.Ln),
                  reads=[res("accD")], writes=[res("accD")])
            P.add("act", lambda e, ts_=ts_: e.activation(out=accD[:, ts_], in_=accD[:, ts_], func=AF.Exp, scale=-1.0),
                  reads=[res("accD")], writes=[res("accD")])
            P.add("dve", lambda e, ts_=ts_: e.tensor_tensor(out=accN[:, ts_], in0=accN[:, ts_], in1=accD[:, ts_], op=ALU.mult),
                  reads=[res("accN"), res("accD")], writes=[res("accN")])
            P.add("dve", lambda e, hp=hp, ts_=ts_: e.tensor_tensor(out=ogT[:, hp, ts_], in0=accN[:, ts_], in1=sg[:, ts_], op=ALU.mult),
                  reads=[res("accN"), res("sg")], writes=[ogr[hp]])

        nd_rot = [0]
        acc4_rot = [0]

        def acc4():
            acc4_rot[0] = (acc4_rot[0] + 1) % 4
            return (0, 1, 3, 4)[acc4_rot[0]]

        for hp in range(8):
            P.add("pool", lambda e, hp=hp: [e.dma_start(out=bias8, in_=db_d[hp])],
                  writes=[res("bias8")], dma_res=res("bias8"), ndma=1)
            P.add("act", lambda e: e.activation(out=bias8[:, :], in_=bias8[:, :], func=AF.Identity, scale=8.0),
                  reads=[res("bias8")], writes=[res("bias8")])
            for g, dil in enumerate(DILS):
                u = hp * 3 + g
                if u + 2 < 24:
                    wdma1(u + 2)
                wk, wres, _ = slot1(u)
                nb = (S // dil) // 128
                for which in range(3):
                    for tile in range(4):
                        bk = acc4()
                        proj_fm(wk, which * 128, tile, bk, wres)
                        if which == 0:
                            o, i_ = perm_views(qT1, tile, dil, ps[bk][:, :])
                            P.add("dve", lambda e, o=o, i_=i_: e.tensor_copy(out=o, in_=i_),
                                  reads=[pr[bk]], writes=[res("qT1")])
                            if g == 0 and hp > 0:
                                emit_combine(hp - 1, tile)
                        elif which == 1:
                            o, i_ = perm_views(kTA, tile, dil, ps[bk][:, :])
                            P.add("act", lambda e, o=o, i_=i_: e.activation(out=o[0:64], in_=i_[0:64], func=AF.Identity),
                                  reads=[pr[bk]], writes=[res("kT1")])
                            o2, i2 = perm_views(kTB, tile, dil, ps[bk][:, :])
                            P.add("dve", lambda e, o2=o2, i2=i2: e.tensor_copy(out=o2[64:128], in_=i2[64:128]),
                                  reads=[pr[bk]], writes=[res("kT1")])
                        else:
                            o, i_ = perm_views(vT1, tile, dil, ps[bk][:, :])
                            P.add("act", lambda e, o=o, i_=i_: e.activation(out=o, in_=i_, func=AF.Identity),
                                  reads=[pr[bk]], writes=[res("vT1")])
                for hb in range(2):
                    tb = 7 if hb == 0 else 2
                    pv = ps[tb][:, :].bitcast(BF16)
                    for j in range(8):
                        B = hb * 8 + j
                        P.add("pe", lambda e, B=B, j=j, pv=pv: e.transpose(
                            pv[:, j * 128:(j + 1) * 128], vT1[:, B * 128:(B + 1) * 128], ident),
                            reads=[res("vT1"), res("cstb")], writes=[pr[tb]])
                    P.add("dve", lambda e, hb=hb, pv=pv: e.tensor_copy(
                        out=vtm1[:, hb * 8:(hb + 1) * 8, :].rearrange("p s d -> p (s d)"), in_=pv[:, 0:1024]),
                        reads=[pr[tb]], writes=[res("vtm1")])

                if g == 2:
                    for tile in range(4):
                        tsl = slice(tile * 512, (tile + 1) * 512)
                        bk = acc4()
                        proj_fm(wgate, 0, tile, bk, rgate)
                        P.add("act", lambda e, bk=bk, tsl=tsl: e.activation(out=sg[:, tsl], in_=ps[bk][:, :], func=AF.Silu),
                              reads=[pr[bk]], writes=[res("sg")])
                    if hp + 1 < 8:
                        gdma(hp + 1)
                    if hp == 7:
                        load_wout(dwo_d, 1, list(range(8)))
                def emit_S(B):
                    n = B % nb
                    m0 = 0 if n > 0 else 1
                    sbk = 3 + (B % 2)
                    pts = B % 2
                    Sv = ps[sbk][:, :].rearrange("p (h m q) -> p h m q", h=2, m=2)
                    for hh in range(2):
                        kT_h = kTA if hh == 0 else kTB
                        P.add("pe", lambda e, Sv=Sv, hh=hh, m0=m0, g=g: e.matmul(
                            Sv[:, hh, m0:2, :], lhsT=ident, rhs=bv[:, g, hh, m0:2, :], start=True, stop=False),
                            reads=[res("bias8"), res("cstb")], writes=[pr[sbk]])
                        for mi in range(m0, 2):
                            m = B - 1 + mi
                            P.add("pe", lambda e, Sv=Sv, hh=hh, mi=mi, m=m, B=B, kT_h=kT_h: e.matmul(
                                Sv[:, hh, mi, :], lhsT=kT_h[:, m * 128:(m + 1) * 128], rhs=qT1[:, B * 128:(B + 1) * 128],
                                start=False, stop=(mi == 1)),
                                reads=[res("kT1"), res("qT1")], writes=[pr[sbk]])
                    PTv = PT[:, pts, :].rearrange("p (h m q) -> p h m q", h=2, m=2)
                    P.add("act", lambda e, Sv=Sv, PTv=PTv, m0=m0: e.activation(
                        out=PTv[:, :, m0:2, :], in_=Sv[:, :, m0:2, :], func=AF.Exp, scale=0.125),
                        reads=[pr[sbk]], writes=[res("PT%d" % pts)])

                def emit_PV(B, nbk, dbk):
                    n = B % nb
                    m0 = 0 if n > 0 else 1
                    pts = B % 2
                    PTv = PT[:, pts, :].rearrange("p (h m q) -> p h m q", h=2, m=2)
                    slot = B % 4
                    for hh in range(2):
                        for (obk, use_v) in ((nbk, True), (dbk, False)):
                            for mi in range(m0, 2):
                                m = B - 1 + mi
                                lhs = vtm1[:, m, hh * 64:(hh + 1) * 64] if use_v else ones64
                                P.add("pe", lambda e, obk=obk, hh=hh, mi=mi, lhs=lhs, PTv=PTv, slot=slot, m0=m0: e.matmul(
                                    ps[obk][hh * 64:(hh + 1) * 64, slot * 128:(slot + 1) * 128], lhsT=lhs, rhs=PTv[:, hh, mi, :],
                                    start=(mi == m0), stop=(mi == 1)),
                                    reads=[res("vtm1"), res("PT%d" % pts), res("cstb")], writes=[pr[obk]])

                emit_S(0)
                for B in range(16):
                    if B % 4 == 0:
                        nd_rot[0] ^= 1
                        nbk, dbk = (5, 7) if nd_rot[0] else (6, 2)
                    if B + 1 < 16:
                        emit_S(B + 1)
                    emit_PV(B, nbk, dbk)
                    if B % 4 == 3:
                        q4 = B // 4
                        for (obk, acc, aname) in ((nbk, accN, "accN"), (dbk, accD, "accD")):
                            if dil == 1:
                                dst = acc[:, q4 * 512:(q4 + 1) * 512]
                                src = ps[obk][:, :]
                            elif dil == 4:
                                dst = acc[:, :].rearrange("p (i r) -> p r i", r=4)[:, q4, :]
                                src = ps[obk][:, :]
                            else:
                                dst = acc[:, :].rearrange("p (i r) -> p r i", r=16)[:, q4 * 4:(q4 + 1) * 4, :]
                                src = ps[obk][:, :].rearrange("p (r i) -> p r i", r=4)
                            if g == 0:
                                P.add("dve", lambda e, dst=dst, src=src: e.tensor_copy(out=dst, in_=src),
                                      reads=[pr[obk]], writes=[res(aname)])
                            else:
                                P.add("dve", lambda e, dst=dst, src=src: e.tensor_tensor(out=dst, in0=src, in1=dst, op=ALU.add),
                                      reads=[pr[obk], res(aname)], writes=[res(aname)])
        for t_ in range(4):
            emit_combine(7, t_)
        outproj(1, b, wo_view, nxt=(("final",) if final else None),
                alias={"fs_sq": ["accD"], "hp_t": ["qT1", "kT1", "vT1"], "hp_rstd": ["vtm1"]})

    def final_norm(b):
        if final:
            P.barrier()
        for tile in range(4):
            tsl = slice(tile * 512, (tile + 1) * 512)
            half = tile % 2
            stg = ogT_f[:, half * 4096:(half + 1) * 4096].rearrange("p (k n) -> p k n", k=8)
            sres = ogr[half * 4:(half + 1) * 4]
            if final:
                rs, rres = stats(tile)
                for hf in range(2):
                    P.add("dve", lambda e, hf=hf, tsl=tsl, rs=rs: e.tensor_tensor(
                        out=hp_t[hf], in0=xT[:, 4 * hf:4 * hf + 4, tsl], in1=bc(rs, 1, 4), op=ALU.mult),
                        reads=xr2[tile][4 * hf:4 * hf + 4] + [rres], writes=[res("hp_t%d" % hf)])
                    for ci in range(4):
                        c = 4 * hf + ci
                        if ci % 2 == 0:
                            P.add("act", lambda e, c=c, ci=ci, hf=hf, stg=stg: e.activation(
                                out=stg[:, c, :], in_=hp_t[hf][:, ci, :], func=AF.Identity, scale=fgT[:, c:c + 1]),
                                reads=[res("hp_t%d" % hf), res("smallin")], writes=sres)
                        else:
                            P.add("dve", lambda e, c=c, ci=ci, hf=hf, stg=stg: e.tensor_scalar(
                                out=stg[:, c, :], in0=hp_t[hf][:, ci, :], scalar1=fgT[:, c:c + 1], scalar2=None, op0=ALU.mult),
                                reads=[res("hp_t%d" % hf), res("smallin")], writes=sres)
                st_ops.append(P.add("sync", lambda e, tsl=tsl, stg=stg: [e.dma_start(out=outT_d[b, :, :, tsl], in_=stg)],
                                    reads=sres, dma_res=res("outst%d" % half), ndma=1))
            else:
                st_ops.append(P.add("sync", lambda e, tsl=tsl: [e.dma_start(out=outT_d[b, :, :, tsl], in_=xT[:, :, tsl])],
                                    reads=xr, dma_res=res("outst%d" % half), ndma=1))

    for b in range(NB):
        for t_ in range(4):
            P.add("pool", lambda e, b=b, t_=t_: [e.dma_start(out=xT[:, :, t_ * 512:(t_ + 1) * 512], in_=xT_d[b, :, :, t_ * 512:(t_ + 1) * 512])],
                  writes=xr2[t_], dma_res=res("xload%d" % t_), ndma=1)
        if 0 in layers:
            gla_layer(b)
        if 1 in layers:
            dsw_layer(b)
        if not (1 in layers and final):
            final_norm(b)
    i = P.add("sync", None, reads=[], writes=[])
    P.ops[i].deps = list(st_ops)

    P.emit(nc, st)
    st.close()
    return nc, P


def _consts():
    c = np.zeros((128, 512), np.float32)
    c[:, 0:128] = np.eye(128, dtype=np.float32)
    j = np.arange(128)[:, None]
    t = np.arange(128)[None, :]
    same = (j // 64) == (t // 64)
    tri = same & (j <= t)
    c[:, 128:256] = np.where(tri, -1.0 / 16.0, 0.0)
    c[:, 256:384] = np.where(tri, 1.0, 0.0)
    c[:, 384:512] = 1.0
    return c


_CACHE = {}


def _get(key):
    if key not in _CACHE:
        layers, final = key
        _CACHE[key] = build(layers=layers, final=final)[0]
    return _CACHE[key]


def _fm(a):
    return a


def kernel(x, c, ada_w, ada_b, norm_g, gla_w_in, gla_w_alpha, gla_b_alpha, gla_norm_g,
           gla_w_out, dsw_w_in, dsw_w_out, rel_bias, final_g, _mode="fused"):
    f = np.float32
    x = np.asarray(x, f)
    ncore = 8
    common = {
        "ada_w": np.ascontiguousarray(np.asarray(ada_w, f).reshape(2, 8, 128, 3072).transpose(0, 2, 1, 3)),
        "ada_bT": np.ascontiguousarray(np.asarray(ada_b, f).reshape(2, 24, 128).transpose(2, 0, 1)),
        "norm_gT": np.ascontiguousarray(np.asarray(norm_g, f).reshape(2, 8, 128).transpose(2, 0, 1)),
        "final_gT": np.ascontiguousarray(np.asarray(final_g, f).reshape(8, 128).T),
        "consts": _consts(),
    }
    gw = np.asarray(gla_w_in, f)[0]
    wh = np.empty((4, 1024, 768), f)
    for h in range(4):
        wh[h, :, 0:128] = gw[:, h * 128:(h + 1) * 128]
        wh[h, :, 128:256] = gw[:, 512 + h * 128:512 + (h + 1) * 128]
        wh[h, :, 256:512] = gw[:, 1024 + h * 256:1024 + (h + 1) * 256]
        wh[h, :, 512:768] = gw[:, 2064 + h * 256:2064 + (h + 1) * 256]
    l0 = {
        "gla_wh": np.ascontiguousarray(wh.reshape(4, 8, 128, 3, 256).transpose(0, 3, 2, 1, 4)),
        "gla_wlr": np.ascontiguousarray(gw[:, 2048:2064].reshape(8, 128, 16).transpose(1, 0, 2)),
        "gla_walpha": np.ascontiguousarray(np.concatenate([np.asarray(gla_w_alpha, f)[0], np.asarray(gla_b_alpha, f)], axis=0)),
        "gla_gnT": np.ascontiguousarray(np.asarray(gla_norm_g, f)[0].reshape(2, 128).T),
        "gla_wo": np.ascontiguousarray(np.asarray(gla_w_out, f)[0].reshape(8, 128, 1024).transpose(1, 0, 2)),
    }
    dw = np.asarray(dsw_w_in, f)[0]
    wd = np.empty((8, 1024, 1280), f)
    for hp in range(8):
        for g in range(3):
            for which in range(3):
                src = g * 3072 + which * 1024 + hp * 128
                wd[hp, :, g * 384 + which * 128:g * 384 + (which + 1) * 128] = dw[:, src:src + 128]
        wd[hp, :, 1152:1280] = dw[:, 9216 + hp * 128:9216 + (hp + 1) * 128]
    l1 = {
        "dsw_wh": np.ascontiguousarray(wd[:, :, 0:1152].reshape(8, 8, 128, 3, 384).transpose(0, 3, 2, 1, 4)),
        "dsw_wg": np.ascontiguousarray(wd[:, :, 1152:1280].reshape(8, 8, 128, 128).transpose(0, 2, 1, 3)),
        "dsw_wo": np.ascontiguousarray(np.asarray(dsw_w_out, f)[0].reshape(8, 128, 1024).transpose(1, 0, 2)),
        "dsw_bias": bias_tables(np.asarray(rel_bias, f)).reshape(8, 128, 1536),
    }
    cT = np.asarray(c, f)

    def xT_of(arr, i):
        a = arr[2 * i:2 * i + 2]
        return np.ascontiguousarray(a.reshape(NB, S, 8, 128).transpose(0, 3, 2, 1))

    def cT_of(i):
        return np.ascontiguousarray(cT[2 * i:2 * i + 2].reshape(NB, 8, 128).transpose(2, 1, 0))

    def run(nc, extra, xin):
        maps = []
        for i in range(ncore):
            m = dict(common)
            m.update(extra)
            m["xT"] = xT_of(xin, i)
            m["cT"] = cT_of(i)
            maps.append(m)
        r = run_bass_kernel_spmd(nc, maps, core_ids=list(range(ncore)))
        out = np.empty((16, S, D), f)
        for i in range(ncore):
            o = np.asarray(r.results[i]["outT"])
            out[2 * i:2 * i + 2] = o.transpose(0, 3, 2, 1).reshape(NB, S, D)
        return out

    if _mode == "fused":
        ex = dict(l0)
        ex.update(l1)
        return run(_get(((0, 1), True)), ex, x)
    x1 = run(_get(((0,), False)), l0, x)
    if _mode == "l0":
        return x1
    return run(_get(((1,), True)), l1, x1)
```
